# Optimizing a Trainium2 kernel written in Bass

```python
import math
import jax, jax.numpy as jnp
from jax import lax
import numpy as np

D_MODEL = 1024
BATCH = 2
SEQ = 16384
DEPTH = 2
DEC_BATCH = 4
DEC_SEQ = 8192
PAST_LEN = 128

N_MEM = 256
EPS = 1e-6
HA = 4
DK_A = 128
DV_A = 256
CHUNK = 128
HB = 8
DH_B = 64
QBLOCK = 128
N_BUCKETS = 32
MAX_DIST = 128
HC = 4
DC = D_MODEL // HC
D_FF = 4 * D_MODEL
N_A = (DEPTH + 1) // 2
N_B = DEPTH // 2

kernel_name = 'hybrid_mlstm_diffattn_encoder'


def lambda_init_fn(layer):
    return 0.8 - 0.6 * math.exp(-0.3 * layer)


def rmsnorm(x, g):
    xf = x.astype(jnp.float32)
    y = xf * lax.rsqrt(jnp.mean(xf * xf, axis=-1, keepdims=True) + EPS)
    return (y * g.astype(jnp.float32)).astype(x.dtype)


def mlstm_scan(q, k, v, ig, lf):
    B, H, S, dk = q.shape
    dv = v.shape[-1]
    nc = S // CHUNK
    def chunks(a):
        return jnp.moveaxis(a.reshape(a.shape[:2] + (nc, CHUNK) + a.shape[3:]), 2, 0)
    tril = jnp.tril(jnp.ones((CHUNK, CHUNK), dtype=bool))

    def step(carry, inp):
        C, n, m = carry
        qc, kc, vc, igc, lfc = inp
        b = jnp.cumsum(lfc, axis=-1)
        dlog = jnp.where(tril, b[..., :, None] - b[..., None, :] + igc[..., None, :], -jnp.inf)
        inter = b + m[..., None]
        m_t = jnp.maximum(inter, jnp.max(dlog, axis=-1))
        s = jnp.einsum('bhtd,bhsd->bhts', qc, kc) * jnp.exp(dlog - m_t[..., None])
        iw = jnp.exp(inter - m_t)
        num = iw[..., None] * jnp.einsum('bhtd,bhde->bhte', qc, C) + jnp.einsum('bhts,bhse->bhte', s, vc)
        den = iw * jnp.einsum('bhtd,bhd->bht', qc, n) + jnp.sum(s, axis=-1)
        h = num / jnp.maximum(jnp.abs(den), jnp.exp(-m_t))[..., None]
        bl = b[..., -1]
        wlog = bl[..., None] - b + igc
        m_new = jnp.maximum(bl + m, jnp.max(wlog, axis=-1))
        w = jnp.exp(wlog - m_new[..., None])
        decay = jnp.exp(bl + m - m_new)
        C_new = decay[..., None, None] * C + jnp.einsum('bhs,bhsd,bhse->bhde', w, kc, vc)
        n_new = decay[..., None] * n + jnp.einsum('bhs,bhsd->bhd', w, kc)
        return (C_new, n_new, m_new), h

    init = (jnp.zeros((B, H, dk, dv), jnp.float32), jnp.zeros((B, H, dk), jnp.float32), jnp.zeros((B, H), jnp.float32))
    _, hs = lax.scan(step, init, (chunks(q), chunks(k), chunks(v), chunks(ig), chunks(lf)))
    return jnp.moveaxis(hs, 0, 2).reshape(B, H, S, dv)


def mlstm_mixer(xn, w_in, w_gate, b_gate, norm_g, w_out):
    B, S, _ = xn.shape
    q, k, v, o = jnp.split(xn @ w_in, [HA * DK_A, 2 * HA * DK_A, 2 * HA * DK_A + HA * DV_A], axis=-1)
    def heads(a, d):
        return a.reshape(B, S, HA, d).transpose(0, 2, 1, 3).astype(jnp.float32)
    q = heads(q, DK_A)
    k = heads(k, DK_A) * (DK_A ** -0.5)
    v = heads(v, DV_A)
    gates = (xn @ w_gate + b_gate).astype(jnp.float32).reshape(B, S, 4, HA).transpose(2, 0, 3, 1)
    ig_f, fg_f, ig_b, fg_b = gates[0], gates[1], gates[2], gates[3]
    h_f = mlstm_scan(q, k, v, ig_f, jax.nn.log_sigmoid(fg_f))
    fl = lambda a: jnp.flip(a, axis=2)
    h_b = fl(mlstm_scan(fl(q), fl(k), fl(v), fl(ig_b), fl(jax.nn.log_sigmoid(fg_b))))
    h = h_f + h_b
    hf = h * lax.rsqrt(jnp.mean(h * h, axis=-1, keepdims=True) + EPS)
    hf = hf.transpose(0, 2, 1, 3).reshape(B, S, HA * DV_A) * norm_g.astype(jnp.float32)
    out = hf * jax.nn.sigmoid(o.astype(jnp.float32))
    return out.astype(xn.dtype) @ w_out


def rel_bucket(rp):
    half = N_BUCKETS // 2
    max_exact = half // 2
    ret = (rp > 0).astype(jnp.int32) * half
    n = jnp.abs(rp)
    nf = jnp.maximum(n, 1).astype(jnp.float32)
    large = max_exact + (jnp.log(nf / max_exact) / math.log(MAX_DIST / max_exact) * (half - max_exact)).astype(jnp.int32)
    large = jnp.minimum(large, half - 1)
    return ret + jnp.where(n < max_exact, n, large)


def diff_attention(xn, w_qkv, lam_vecs, subln_g, w_out, rel_table, lambda_init):
    B, S, _ = xn.shape
    nblk = S // QBLOCK
    q, k, v = jnp.split(xn @ w_qkv, [2 * HB * DH_B, 4 * HB * DH_B], axis=-1)
    q = q.reshape(B, S, HB, 2, DH_B)
    k = k.reshape(B, S, HB, 2, DH_B)
    k1 = k[..., 0, :].transpose(0, 2, 1, 3)
    k2 = k[..., 1, :].transpose(0, 2, 1, 3)
    v = v.reshape(B, S, HB, 2 * DH_B).transpose(0, 2, 1, 3)
    def to_blocks(a):
        return a.reshape(B, nblk, QBLOCK, HB, DH_B).transpose(1, 0, 3, 2, 4)
    q1b = to_blocks(q[..., 0, :])
    q2b = to_blocks(q[..., 1, :])
    lv = lam_vecs.astype(jnp.float32)
    lam = jnp.exp(jnp.sum(lv[0] * lv[1])) - jnp.exp(jnp.sum(lv[2] * lv[3])) + lambda_init
    kpos = jnp.arange(S, dtype=jnp.int32)
    scale = DH_B ** -0.5
    table = rel_table.astype(jnp.float32)

    def block(args):
        q1, q2, q0 = args
        qpos = q0 + jnp.arange(QBLOCK, dtype=jnp.int32)
        bias = jnp.transpose(table[rel_bucket(kpos[None, :] - qpos[:, None])], (2, 0, 1))[None]
        p1 = jax.nn.softmax(jnp.einsum('bhqd,bhkd->bhqk', q1, k1).astype(jnp.float32) * scale + bias, axis=-1)
        p2 = jax.nn.softmax(jnp.einsum('bhqd,bhkd->bhqk', q2, k2).astype(jnp.float32) * scale + bias, axis=-1)
        a = (p1 - lam * p2).astype(v.dtype)
        return jnp.einsum('bhqk,bhke->bhqe', a, v)

    o = lax.map(block, (q1b, q2b, jnp.arange(nblk, dtype=jnp.int32) * QBLOCK))
    o = rmsnorm(o, subln_g) * (1.0 - lambda_init)
    o = o.transpose(1, 0, 3, 2, 4).reshape(B, S, HB * 2 * DH_B)
    return o @ w_out


def cross_attention(xn, memn, w_q, w_kv, w_out):
    B, S, _ = xn.shape
    M = memn.shape[1]
    q = (xn @ w_q).reshape(B, S, HC, DC)
    k, v = jnp.split(memn @ w_kv, 2, axis=-1)
    k = k.reshape(B, M, HC, DC)
    v = v.reshape(B, M, HC, DC)
    p = jax.nn.softmax(jnp.einsum('bshd,bmhd->bhsm', q, k).astype(jnp.float32) * (DC ** -0.5), axis=-1).astype(v.dtype)
    o = jnp.einsum('bhsm,bmhd->bshd', p, v).reshape(B, S, HC * DC)
    return o @ w_out


def squared_relu_mlp(xn, w1, w2):
    return jnp.square(jax.nn.relu(xn @ w1)) @ w2


def encoder(x, mem, g_mix, g_cross, g_mem, g_mlp, g_final, a_w_in, a_w_gate, a_b_gate, a_norm_g, a_w_out,
            b_w_qkv, b_lambda, b_subln_g, b_w_out, rel_bias, c_w_q, c_w_kv, c_w_out, f_w1, f_w2):
    for i in range(DEPTH):
        xn = rmsnorm(x, g_mix[i])
        j = i // 2
        if i % 2 == 0:
            x = x + mlstm_mixer(xn, a_w_in[j], a_w_gate[j], a_b_gate[j], a_norm_g[j], a_w_out[j])
        else:
            x = x + diff_attention(xn, b_w_qkv[j], b_lambda[j], b_subln_g[j], b_w_out[j], rel_bias, lambda_init_fn(i))
        x = x + cross_attention(rmsnorm(x, g_cross[i]), rmsnorm(mem, g_mem[i]), c_w_q[i], c_w_kv[i], c_w_out[i])
        x = x + squared_relu_mlp(rmsnorm(x, g_mlp[i]), f_w1[i], f_w2[i])
    return rmsnorm(x, g_final)


def setup_inputs(seed: int = 0) -> dict:
    key = jax.random.key(seed)
    ks = jax.random.split(key, 32)
    nrm = lambda k, shape, s: jax.random.normal(k, shape, jnp.float32) * s
    gain = lambda k, shape: 1.0 + nrm(k, shape, 0.02)
    gate_off = jnp.array([0.0, 3.0, 0.0, 3.0], jnp.float32)[None, :, None]
    gate_scale = jnp.array([0.1, 0.5, 0.1, 0.5], jnp.float32)[None, :, None]
    a_b_gate = (gate_off + gate_scale * jax.random.normal(ks[9], (N_A, 4, HA), jnp.float32)).reshape(N_A, 4 * HA)
    return {
        'x_prompt': nrm(ks[0], (BATCH, SEQ, D_MODEL), 1.0),
        'x_sample': nrm(ks[1], (DEC_BATCH, DEC_SEQ, D_MODEL), 1.0),
        'mem_prompt': nrm(ks[2], (BATCH, N_MEM, D_MODEL), 1.0),
        'mem_sample': nrm(ks[3], (DEC_BATCH, N_MEM, D_MODEL), 1.0),
        'g_mix': gain(ks[4], (DEPTH, D_MODEL)),
        'g_cross': gain(ks[5], (DEPTH, D_MODEL)),
        'g_mem': gain(ks[6], (DEPTH, D_MODEL)),
        'g_mlp': gain(ks[7], (DEPTH, D_MODEL)),
        'g_final': gain(ks[8], (D_MODEL,)),
        'a_w_in': nrm(ks[10], (N_A, D_MODEL, 2 * HA * DK_A + 2 * HA * DV_A), D_MODEL ** -0.5),
        'a_w_gate': nrm(ks[11], (N_A, D_MODEL, 4 * HA), D_MODEL ** -0.5),
        'a_b_gate': a_b_gate,
        'a_norm_g': gain(ks[12], (N_A, HA * DV_A)),
        'a_w_out': nrm(ks[13], (N_A, HA * DV_A, D_MODEL), (HA * DV_A) ** -0.5),
        'b_w_qkv': nrm(ks[14], (N_B, D_MODEL, 6 * HB * DH_B), D_MODEL ** -0.5),
        'b_lambda': nrm(ks[15], (N_B, 4, DH_B), 0.1),
        'b_subln_g': gain(ks[16], (N_B, 2 * DH_B)),
        'b_w_out': nrm(ks[17], (N_B, 2 * HB * DH_B, D_MODEL), (2 * HB * DH_B) ** -0.5),
        'rel_bias': nrm(ks[18], (N_BUCKETS, HB), 0.1),
        'c_w_q': nrm(ks[19], (DEPTH, D_MODEL, HC * DC), D_MODEL ** -0.5),
        'c_w_kv': nrm(ks[20], (DEPTH, D_MODEL, 2 * HC * DC), D_MODEL ** -0.5),
        'c_w_out': nrm(ks[21], (DEPTH, HC * DC, D_MODEL), (HC * DC) ** -0.5),
        'f_w1': nrm(ks[22], (DEPTH, D_MODEL, D_FF), D_MODEL ** -0.5),
        'f_w2': nrm(ks[23], (DEPTH, D_FF, D_MODEL), D_FF ** -0.5),
    }


def reference(x_prompt, x_sample, mem_prompt, mem_sample, g_mix, g_cross, g_mem, g_mlp, g_final,
              a_w_in, a_w_gate, a_b_gate, a_norm_g, a_w_out, b_w_qkv, b_lambda, b_subln_g, b_w_out,
              rel_bias, c_w_q, c_w_kv, c_w_out, f_w1, f_w2):
    y_prompt = encoder(x_prompt, mem_prompt, g_mix, g_cross, g_mem, g_mlp, g_final, a_w_in, a_w_gate, a_b_gate,
                       a_norm_g, a_w_out, b_w_qkv, b_lambda, b_subln_g, b_w_out, rel_bias, c_w_q, c_w_kv, c_w_out,
                       f_w1, f_w2)
    y_sample = encoder(x_sample, mem_sample, g_mix, g_cross, g_mem, g_mlp, g_final, a_w_in, a_w_gate, a_b_gate,
                       a_norm_g, a_w_out, b_w_qkv, b_lambda, b_subln_g, b_w_out, rel_bias, c_w_q, c_w_kv, c_w_out,
                       f_w1, f_w2)
    return (y_prompt, y_sample)
```

```python
import contextlib
import os
import math
import numpy as np
import concourse.bass as bass
import concourse.mybir as mybir
from concourse.bass_utils import run_bass_kernel_spmd

F32 = mybir.dt.float32
BF16 = mybir.dt.bfloat16
ALU = mybir.AluOpType
AF = mybir.ActivationFunctionType
AX = mybir.AxisListType

D = 1024
EPS = 1e-6
NEG = -30000.0
STRICT = True


class Buf:
    __slots__ = ("w", "r", "dsem", "dcnt", "excl")

    def __init__(self, excl=False):
        self.w = None
        self.r = {}
        self.dsem = None
        self.dcnt = 0
        self.excl = excl


def PBuf():
    return Buf(excl=True)


class Rec:
    def __init__(self):
        self.call = None

    def __getattr__(self, name):
        def f(*a, **k):
            self.call = (name, a, k)
            return self
        return f


def _record(fn):
    r = Rec()
    fn(r)
    assert r.call is not None
    return r.call


def _play(eng, call):
    name, a, k = call
    return getattr(eng, name)(*a, **k)


class Sched:
    ENG = ["pe", "act", "dve", "pool", "sp"]

    def __init__(self, nc, stack):
        self.nc = nc
        self.stack = stack
        self.q = {e: [] for e in self.ENG}
        self.cnt = {e: 0 for e in self.ENG}
        self.sem = {e: stack.enter_context(nc.semaphore("s_" + e)) for e in self.ENG}
        self.bar = stack.enter_context(nc.semaphore("s_bar"))
        self.bar_cnt = 0
        self.waited = {}
        self.dbufs = []
        self.nsem = 0

    def dbuf(self):
        b = Buf()
        self.nsem += 1
        b.dsem = self.stack.enter_context(self.nc.semaphore("d%d" % self.nsem))
        self.dbufs.append(b)
        return b

    def _wait(self, e, ev):
        sem, val, key = ev
        if self.waited.get((e, key), 0) >= val:
            return
        self.waited[(e, key)] = val
        self.q[e].append(lambda eng, sem=sem, val=val: eng.wait_ge(sem, val))

    def op(self, e, fn, reads=(), writes=(), dma=None):
        deps = []
        xr = [b for b in reads if b.excl]
        if xr:
            reads = [b for b in reads if not b.excl]
            writes = list(writes) + xr
        for b in reads:
            if b.w is not None:
                deps.append(b.w)
        for b in writes:
            if b.w is not None:
                deps.append(b.w)
            deps.extend(b.r.values())
        for ev in deps:
            if ev[2] == e and (e == "pe" or not STRICT):
                continue
            self._wait(e, ev)
        if dma is None:
            self.cnt[e] += 1
            sem = self.sem[e]
            ev = (sem, self.cnt[e], e)
            call = _record(fn)
            self.q[e].append(lambda eng, call=call, sem=sem: _play(eng, call).then_inc(sem, 1))
        else:
            dma.dcnt += 16
            sem = dma.dsem
            ev = (sem, dma.dcnt, ("d", id(dma)))
            call = _record(fn)
            self.q[e].append(lambda eng, call=call, sem=sem: _play(eng, call).then_inc(sem, 16))
        for b in writes:
            b.w = ev
            b.r = {}
        for b in reads:
            b.r[ev[2]] = ev

    def barrier(self, dummy_src, dummy_dst):
        for e in ["pe", "act", "dve", "pool"]:
            if self.cnt[e]:
                self._wait("sp", (self.sem[e], self.cnt[e], e))
        for b in self.dbufs:
            if b.dcnt:
                self._wait("sp", (b.dsem, b.dcnt, ("d", id(b))))
        self.bar_cnt += 16
        bar = self.bar
        self.q["sp"].append(lambda eng: eng.dma_start(out=dummy_dst, in_=dummy_src).then_inc(bar, 16))
        for e in self.ENG:
            self._wait(e, (bar, self.bar_cnt, "bar"))

    def emit(self):
        nc = self.nc
        q = self.q
        with nc.Block() as block:
            @block.sync
            def _(eng):
                for f in q["sp"]:
                    f(eng)

            @block.tensor
            def _(eng):
                for f in q["pe"]:
                    f(eng)

            @block.scalar
            def _(eng):
                for f in q["act"]:
                    f(eng)

            @block.vector
            def _(eng):
                for f in q["dve"]:
                    f(eng)

            @block.gpsimd
            def _(eng):
                for f in q["pool"]:
                    f(eng)


def build_program(L, dbg=False):
    NT = L // 128
    NQ = L // 2
    NQT = NQ // 128
    nc = bass.Bass("TRN2", target_bir_lowering=False)
    stack = contextlib.ExitStack()

    def din(name, shape, dt=F32):
        return nc.dram_tensor(name, list(shape), dt, kind="ExternalInput").ap()

    def dscr(name, shape, dt=F32):
        return nc.dram_tensor(name, list(shape), dt).ap()

    x_in = din("x", [L, D])
    mem_in = din("mem", [2, 256, D])
    gains = din("gains", [9, D])
    a_w_in = din("a_w_in", [D, 3072])
    a_w_gate = din("a_w_gate", [D, 16])
    a_b_gate = din("a_b_gate", [1, 16])
    a_norm_g = din("a_norm_g", [1, D])
    a_w_out = din("a_w_out", [D, D])
    b_w_qkv = din("b_w_qkv", [D, 3072])
    b_lambda = din("b_lambda", [1, 256])
    b_subln_g = din("b_subln_g", [128, 1])
    b_w_out = din("b_w_out", [D, D])
    rel_bias = din("rel_bias", [32, 8])
    c_w_q = din("c_w_q", [2, D, D])
    c_w_kv = din("c_w_kv", [2, D, 2048])
    c_w_out = din("c_w_out", [2, D, D])
    f_w1 = din("f_w1", [2, D, 4096])
    f_w2 = din("f_w2", [2, 4096, D])
    cmat = din("cmat", [128, 6, 128])
    onehot = din("onehot", [32, 1408])
    flags = din("flags", [128, 4])
    y_out = nc.dram_tensor("y", [NQ, D], F32, kind="ExternalOutput").ap()

    s_qT = dscr("s_qT", [4, 128, L], BF16)
    s_kT = dscr("s_kT", [4, 128, L], BF16)
    s_k = dscr("s_k", [L, 512], BF16)
    s_v = dscr("s_v", [L, 4 * 257], BF16)
    s_so = dscr("s_so", [L, D], F32)
    s_g = dscr("s_g", [L, 16], F32)
    s_hf = dscr("s_hf", [L, D], F32)
    s_hb = dscr("s_hb", [L, D], F32)
    s_x1 = dscr("s_x1", [L, D], F32)
    s_x2 = dscr("s_x2", [L, D], F32)
    s_KT = dscr("s_KT", [8, 128, L], BF16)
    s_V = dscr("s_V", [8, 128, NT, 128], BF16)
    s_QT = dscr("s_QT", [8, 128, NQ], BF16)
    s_oT = dscr("s_oT", [8, 128, NQ], BF16)
    s_x3 = dscr("s_x3", [NQ, D], F32)
    s_fv = dscr("s_fv", [8, 1408], F32)
    s_dummy = dscr("s_dummy", [2, 16], F32)

    S = Sched(nc, stack)
    KSTOP = int(os.environ.get("KSTOP", "99"))
    BSTEP = int(os.environ.get("BSTEP", "99"))
    phase_no = [0]

    def phase_on():
        phase_no[0] += 1
        return phase_no[0] <= KSTOP

    uid = [0]

    def sb(name, shape, dt, st=None):
        uid[0] += 1
        return (st or stack).enter_context(nc.sbuf_tensor("%s_%d" % (name, uid[0]), list(shape), dt))

    def ps(name, shape, dt, st=None):
        uid[0] += 1
        return (st or stack).enter_context(nc.psum_tensor("%s_%d" % (name, uid[0]), list(shape), dt))

    cm = sb("cm", [128, 6, 128], F32)
    cmB = S.dbuf()
    identb = sb("identb", [128, 128], BF16)
    onesb = sb("onesb", [128, 128], BF16)
    onesf = sb("onesf", [128, 128], F32)
    mUb = sb("mUb", [128, 128], BF16)
    mLb = sb("mLb", [128, 128], BF16)
    flg = sb("flg", [128, 4], F32)
    flgB = S.dbuf()
    constB = Buf()
    S.op("sp", lambda e: e.dma_start(out=cm[:, :, :], in_=cmat), writes=[cmB], dma=cmB)
    S.op("sp", lambda e: e.dma_start(out=flg[:, :], in_=flags), writes=[flgB], dma=flgB)
    S.op("dve", lambda e: e.tensor_copy(out=identb[:, :], in_=cm[:, 0, :]), reads=[cmB], writes=[constB])
    S.op("dve", lambda e: e.tensor_copy(out=mUb[:, :], in_=cm[:, 4, :]), reads=[cmB], writes=[constB])
    S.op("dve", lambda e: e.tensor_copy(out=mLb[:, :], in_=cm[:, 5, :]), reads=[cmB], writes=[constB])
    S.op("dve", lambda e: e.memset(onesb[:, :], 1.0), writes=[constB])
    S.op("dve", lambda e: e.memset(onesf[:, :], 1.0), writes=[constB])
    identF = cm[:, 0, :]
    Jf = cm[:, 1, :]
    Uf = cm[:, 2, :]
    Lf = cm[:, 3, :]
    CB = [cmB, constB, flgB]

    stg = [sb("stg%d" % i, [128, 1024], F32) for i in range(2)]
    stgB = [S.dbuf() for _ in range(2)]
    stg_i = [0]
    cast_eng = ["act", "dve", "pool"]

    def load_w(w_ap, K, N, dst, dstB, n0dst=0):
        for k in range(K // 128):
            for n0 in range(0, N, 1024):
                n = min(1024, N - n0)
                i = stg_i[0] % 2
                ce = cast_eng[stg_i[0] % 3]
                stg_i[0] += 1
                S.op("sp", lambda e, i=i, k=k, n0=n0, n=n: e.dma_start(
                    out=stg[i][:, :n], in_=w_ap[k * 128:(k + 1) * 128, n0:n0 + n]),
                    writes=[stgB[i]], dma=stgB[i])
                if ce == "act":
                    S.op("act", lambda e, i=i, k=k, n0=n0, n=n: e.copy(
                        out=dst[:, k, n0dst + n0:n0dst + n0 + n], in_=stg[i][:, :n]),
                        reads=[stgB[i]], writes=[dstB])
                else:
                    S.op(ce, lambda e, i=i, k=k, n0=n0, n=n: e.tensor_copy(
                        out=dst[:, k, n0dst + n0:n0dst + n0 + n], in_=stg[i][:, :n]),
                        reads=[stgB[i]], writes=[dstB])

    def load_gain(idx, dst, dstB):
        S.op("sp", lambda e: e.dma_start(out=dst[:, :], in_=gains[idx:idx + 1, :].partition_broadcast(128)),
             writes=[dstB], dma=dstB)

    def mk_norm_tiles(st, tag):
        t = {}
        t["junk"] = sb("junk" + tag, [128, D], BF16, st)
        t["junkB"] = Buf()
        t["ss"] = sb("ss" + tag, [128, 4], F32, st)
        t["ssB"] = Buf()
        t["xn"] = sb("xn" + tag, [128, D], BF16, st)
        t["xnB"] = Buf()
        t["pt"] = ps("pt" + tag, [128, 8, 128], BF16, st)
        t["ptB"] = PBuf()
        return t

    def rmsnorm_T(t, x_sb, xB, g_sb, gB, xnT, xnTB, toff, width=D):
        nchunk = width // 128
        S.op("act", lambda e: e.activation(out=t["junk"][:, :width], in_=x_sb, func=AF.Square,
                                            accum_out=t["ss"][:, 0:1]),
             reads=[xB], writes=[t["junkB"], t["ssB"]])
        S.op("dve", lambda e: e.tensor_scalar(out=t["ss"][:, 1:2], in0=t["ss"][:, 0:1], scalar1=1.0 / width,
                                              scalar2=EPS, op0=ALU.mult, op1=ALU.add),
             reads=[t["ssB"]], writes=[t["ssB"]])
        S.op("act", lambda e: e.sqrt(out=t["ss"][:, 2:3], in_=t["ss"][:, 1:2]), reads=[t["ssB"]], writes=[t["ssB"]])
        S.op("dve", lambda e: e.reciprocal(out=t["ss"][:, 3:4], in_=t["ss"][:, 2:3]),
             reads=[t["ssB"]], writes=[t["ssB"]])
        S.op("dve", lambda e: e.scalar_tensor_tensor(out=t["xn"][:, :width], in0=x_sb, scalar=t["ss"][:, 3:4],
                                                     in1=g_sb, op0=ALU.mult, op1=ALU.mult),
             reads=[xB, t["ssB"], gB], writes=[t["xnB"]])
        for c in range(nchunk):
            S.op("pe", lambda e, c=c: e.transpose(out=t["pt"][:, c, :], in_=t["xn"][:, c * 128:(c + 1) * 128],
                                                  identity=identb[:, :]),
                 reads=[t["xnB"], constB], writes=[t["ptB"]])
        S.op("dve", lambda e: e.tensor_copy(out=xnT[:, :nchunk, toff:toff + 128], in_=t["pt"][:, :nchunk, :]),
             reads=[t["ptB"]], writes=[xnTB])

    def transpose_to(t, src_bf, srcB, dstT, dstTB, toff, nchunk=8):
        for c in range(nchunk):
            S.op("pe", lambda e, c=c: e.transpose(out=t["pt"][:, c, :], in_=src_bf[:, c * 128:(c + 1) * 128],
                                                  identity=identb[:, :]),
                 reads=[srcB, constB], writes=[t["ptB"]])
        S.op("dve", lambda e: e.tensor_copy(out=dstT[:, :nchunk, toff:toff + 128], in_=t["pt"][:, :nchunk, :]),
             reads=[t["ptB"]], writes=[dstTB])

    def barrier():
        S.barrier(s_dummy[0:1, :], s_dummy[1:2, :])

    with contextlib.ExitStack() as st:
      if phase_on():
        W = sb("A_W", [128, 8, 3072 + 512 + 16], BF16, st)
        WB = Buf()
        load_w(a_w_in, D, 3072, W, WB, 0)
        load_w(a_w_gate, D, 16, W, WB, 3072 + 512)
        gm = sb("A_g", [128, D], F32, st)
        gmB = S.dbuf()
        load_gain(0, gm, gmB)
        bg = sb("A_bg", [128, 16], F32, st)
        bgB = S.dbuf()
        S.op("sp", lambda e: e.dma_start(out=bg[:, :], in_=a_b_gate.partition_broadcast(128)), writes=[bgB], dma=bgB)
        nt_ = mk_norm_tiles(st, "A")
        xt = [sb("A_x%d" % i, [128, D], F32, st) for i in range(2)]
        xtB = [S.dbuf() for _ in range(2)]
        xnT = [sb("A_xnT%d" % i, [128, 8, 128], BF16, st) for i in range(2)]
        xnTB = [Buf() for _ in range(2)]
        pfm = [ps("A_pfm%d" % i, [128, 512], F32, st) for i in range(2)]
        pfmB = [PBuf() for _ in range(2)]
        ptm = [ps("A_ptm%d" % i, [128, 512], F32, st) for i in range(2)]
        ptmB = [PBuf() for _ in range(2)]
        oqk = [sb("A_oqk%d" % i, [128, 8, 128], BF16, st) for i in range(2)]
        oqkB = [S.dbuf() for _ in range(2)]
        ok = [sb("A_ok%d" % i, [128, 512], BF16, st) for i in range(2)]
        okB = [S.dbuf() for _ in range(2)]
        ov = [sb("A_ov%d" % i, [128, 4, 257], BF16, st) for i in range(2)]
        ovB = [S.dbuf() for _ in range(2)]
        oso = [sb("A_oso%d" % i, [128, D], F32, st) for i in range(2)]
        osoB = [S.dbuf() for _ in range(2)]
        og = [sb("A_og%d" % i, [128, 16], F32, st) for i in range(2)]
        ogB = [S.dbuf() for _ in range(2)]
        for i in range(2):
            S.op("dve", lambda e, i=i: e.memset(ov[i][:, :, 256:257], 1.0), writes=[ovB[i]])
        kscale = 128.0 ** -0.5

        def loadx(ti):
            i = ti % 2
            S.op("sp", lambda e: e.dma_start(out=xt[i][:, :], in_=x_in[ti * 128:(ti + 1) * 128, :]),
                 writes=[xtB[i]], dma=xtB[i])

        loadx(0)
        pi = 0
        for ti in range(NT):
            i = ti % 2
            if ti + 1 < NT:
                loadx(ti + 1)
            rmsnorm_T(nt_, xt[i][:, :], xtB[i], gm[:, :], gmB, xnT[i], xnTB[i], 0)
            for half in range(2):
                p = pfm[pi % 2]; pB = pfmB[pi % 2]; pi += 1
                for h in range(4):
                    col = half * 512 + h * 128
                    for c in range(8):
                        S.op("pe", lambda e, p=p, h=h, c=c, col=col: e.matmul(
                            p[:, h * 128:(h + 1) * 128], lhsT=W[:, c, col:col + 128], rhs=xnT[i][:, c, :],
                            start=(c == 0), stop=(c == 7)), reads=[WB, xnTB[i]], writes=[pB])
                S.op("act", lambda e, p=p, half=half: e.mul(
                    out=oqk[i][:, half * 4:(half + 1) * 4, :], in_=p[:, :].rearrange("p (h t) -> p h t", h=4),
                    mul=(1.0 if half == 0 else kscale)), reads=[pB], writes=[oqkB[i]])
            p = ptm[0]; pB = ptmB[0]
            for c in range(8):
                S.op("pe", lambda e, p=p, c=c: e.matmul(p[:, :], lhsT=xnT[i][:, c, :], rhs=W[:, c, 512:1024],
                                                        start=(c == 0), stop=(c == 7)),
                     reads=[WB, xnTB[i]], writes=[pB])
            S.op("act", lambda e, p=p: e.mul(out=ok[i][:, :], in_=p[:, :], mul=kscale),
                 reads=[pB], writes=[okB[i]])
            for hv in range(2):
                p = ptm[1]; pB = ptmB[1]
                for c in range(8):
                    S.op("pe", lambda e, p=p, c=c, hv=hv: e.matmul(
                        p[:, :], lhsT=xnT[i][:, c, :], rhs=W[:, c, 1024 + hv * 512:1024 + (hv + 1) * 512],
                        start=(c == 0), stop=(c == 7)), reads=[WB, xnTB[i]], writes=[pB])
                S.op("dve", lambda e, p=p, hv=hv: e.tensor_copy(
                    out=ov[i][:, hv * 2:(hv + 1) * 2, 0:256], in_=p[:, :].rearrange("p (h e) -> p h e", h=2)),
                    reads=[pB], writes=[ovB[i]])
            for ho in range(2):
                p = ptm[0]; pB = ptmB[0]
                for c in range(8):
                    S.op("pe", lambda e, p=p, c=c, ho=ho: e.matmul(
                        p[:, :], lhsT=xnT[i][:, c, :], rhs=W[:, c, 2048 + ho * 512:2048 + (ho + 1) * 512],
                        start=(c == 0), stop=(c == 7)), reads=[WB, xnTB[i]], writes=[pB])
                S.op("act", lambda e, p=p, ho=ho: e.activation(
                    out=oso[i][:, ho * 512:(ho + 1) * 512], in_=p[:, :], func=AF.Sigmoid),
                    reads=[pB], writes=[osoB[i]])
            p = ptm[1]; pB = ptmB[1]
            for c in range(8):
                S.op("pe", lambda e, p=p, c=c: e.matmul(p[:, 0:16], lhsT=xnT[i][:, c, :],
                                                        rhs=W[:, c, 3584:3600], start=(c == 0), stop=(c == 7)),
                     reads=[WB, xnTB[i]], writes=[pB])
            S.op("dve", lambda e, p=p: e.tensor_tensor(out=og[i][:, :], in0=p[:, 0:16], in1=bg[:, :], op=ALU.add),
                 reads=[pB, bgB], writes=[ogB[i]])
            r0 = ti * 128
            S.op("sp", lambda e, r0=r0: e.dma_start(
                out=s_qT[:, :, r0:r0 + 128].rearrange("h p t -> p h t"), in_=oqk[i][:, 0:4, :]),
                reads=[oqkB[i]], dma=oqkB[i])
            S.op("sp", lambda e, r0=r0: e.dma_start(
                out=s_kT[:, :, r0:r0 + 128].rearrange("h p t -> p h t"), in_=oqk[i][:, 4:8, :]),
                reads=[oqkB[i]], dma=oqkB[i])
            S.op("sp", lambda e, r0=r0: e.dma_start(out=s_k[r0:r0 + 128, :], in_=ok[i][:, :]),
                 reads=[okB[i]], dma=okB[i])
            S.op("sp", lambda e, r0=r0: e.dma_start(
                out=s_v[r0:r0 + 128, :], in_=ov[i][:, :, :].rearrange("p h e -> p (h e)")),
                reads=[ovB[i]], dma=ovB[i])
            S.op("sp", lambda e, r0=r0: e.dma_start(out=s_so[r0:r0 + 128, :], in_=oso[i][:, :]),
                 reads=[osoB[i]], dma=osoB[i])
            S.op("sp", lambda e, r0=r0: e.dma_start(out=s_g[r0:r0 + 128, :], in_=og[i][:, :]),
                 reads=[ogB[i]], dma=ogB[i])
        barrier()

    with contextlib.ExitStack() as st:
      if phase_on():
        NB = 3
        qT = [sb("B_qT%d" % i, [128, 4, 128], BF16, st) for i in range(NB)]
        kT = [sb("B_kT%d" % i, [128, 4, 128], BF16, st) for i in range(NB)]
        ktm = [sb("B_k%d" % i, [128, 512], BF16, st) for i in range(NB)]
        vtm = [sb("B_v%d" % i, [128, 4, 257], BF16, st) for i in range(NB)]
        gt = [sb("B_g%d" % i, [128, 16], F32, st) for i in range(NB)]
        inB = [S.dbuf() for _ in range(NB)]
        lg = sb("B_lg", [128, 16], F32, st)
        lgB = Buf()
        r1 = sb("B_r1", [128, 4, 128], F32, st)
        r1B = Buf()
        l2 = sb("B_l2", [128, 4, 128], F32, st)
        l2B = Buf()
        pBt = ps("B_pBt", [128, 4, 128], F32, st)
        pBtB = PBuf()
        pD = ps("B_pD", [128, 4, 128], F32, st)
        pDB = PBuf()
        pS = ps("B_pS", [128, 4, 128], F32, st)
        pSB = PBuf()
        pH = [ps("B_pH%d" % i, [128, 512], F32, st) for i in range(2)]
        pHB = [PBuf() for _ in range(2)]
        pC = [ps("B_pC%d" % i, [128, 512], F32, st) for i in range(2)]
        pCB = [PBuf() for _ in range(2)]
        eBt = sb("B_eBt", [128, 4, 128], F32, st)
        eBtB = Buf()
        bL = sb("B_bL", [128, 4], F32, st)
        bLB = Buf()
        wv = sb("B_w", [128, 4], F32, st)
        wvB = Buf()
        Dm = sb("B_D", [128, 4, 128], F32, st)
        DmB = Buf()
        Pm = sb("B_P", [128, 4, 128], BF16, st)
        PmB = Buf()
        qs = sb("B_qs", [128, 4, 128], BF16, st)
        qsB = Buf()
        vw = sb("B_vw", [128, 4, 257], BF16, st)
        vwB = Buf()
        Cst = sb("B_C", [128, 4, 257], F32, st)
        CstB = Buf()
        Cbf = sb("B_Cbf", [128, 4, 257], BF16, st)
        CbfB = Buf()
        den = sb("B_den", [128, 8], F32, st)
        denB = Buf()
        ho = [sb("B_ho%d" % i, [128, D], F32, st) for i in range(2)]
        hoB = [S.dbuf() for _ in range(2)]

        def loadc(ci, slot):
            r0 = ci * 128
            B_ = inB[slot]
            S.op("sp", lambda e: e.dma_start(out=qT[slot][:, :, :], in_=s_qT[:, :, r0:r0 + 128].rearrange("h p t -> p h t")),
                 writes=[B_], dma=B_)
            S.op("sp", lambda e: e.dma_start(out=kT[slot][:, :, :], in_=s_kT[:, :, r0:r0 + 128].rearrange("h p t -> p h t")),
                 writes=[B_], dma=B_)
            S.op("sp", lambda e: e.dma_start(out=ktm[slot][:, :], in_=s_k[r0:r0 + 128, :]), writes=[B_], dma=B_)
            S.op("sp", lambda e: e.dma_start(out=vtm[slot][:, :, :].rearrange("p h e -> p (h e)"), in_=s_v[r0:r0 + 128, :]),
                 writes=[B_], dma=B_)
            S.op("sp", lambda e: e.dma_start(out=gt[slot][:, :], in_=s_g[r0:r0 + 128, :]), writes=[B_], dma=B_)

        for direction in range(2):
            order = list(range(NT)) if direction == 0 else list(range(NT - 1, -1, -1))
            Mf = Uf if direction == 0 else Lf
            mB = cm[:, 4, :] if direction == 0 else cm[:, 5, :]
            gi0 = 0 if direction == 0 else 8
            s_h = s_hf if direction == 0 else s_hb
            bcol = 127 if direction == 0 else 0
            S.op("dve", lambda e: e.memset(Cst[:, :, :], 0.0), writes=[CstB])
            S.op("dve", lambda e: e.memset(Cbf[:, :, :], 0.0), writes=[CbfB])
            loadc(order[0], 0)
            if NT > 1:
                loadc(order[1], 1)
            for n, ci in enumerate(order):
                sl = n % NB
                if n + 2 < NT:
                    loadc(order[n + 2], (n + 2) % NB)
                IB = inB[sl]
                g_ = gt[sl]
                if n == NT // 2:
                    S.op("dve", lambda e: e.tensor_scalar(out=Cst[:, :, :], in0=Cst[:, :, :], scalar1=flg[:, 0:1],
                                                          scalar2=None, op0=ALU.mult), reads=[CstB, flgB], writes=[CstB])
                    S.op("dve", lambda e: e.tensor_copy(out=Cbf[:, :, :], in_=Cst[:, :, :]), reads=[CstB], writes=[CbfB])
                S.op("act", lambda e: e.activation(out=lg[:, 0:4], in_=g_[:, gi0 + 4:gi0 + 8], func=AF.Exp, scale=-1.0),
                     reads=[IB], writes=[lgB])
                S.op("act", lambda e: e.activation(out=lg[:, 4:8], in_=lg[:, 0:4], func=AF.Ln, bias=1.0),
                     reads=[lgB], writes=[lgB])
                S.op("dve", lambda e: e.tensor_scalar(out=lg[:, 8:12], in0=lg[:, 4:8], scalar1=-1.0, scalar2=None,
                                                      op0=ALU.mult), reads=[lgB], writes=[lgB])
                if BSTEP < 2:
                    continue
                for h in range(4):
                    S.op("dve", lambda e, h=h: e.tensor_scalar(out=r1[:, h, :], in0=Mf, scalar1=lg[:, 8 + h:9 + h],
                                                               scalar2=None, op0=ALU.mult),
                         reads=[lgB, cmB], writes=[r1B])
                    S.op("dve", lambda e, h=h: e.scalar_tensor_tensor(
                        out=l2[:, h, :], in0=identF, scalar=g_[:, gi0 + h:gi0 + h + 1], in1=r1[:, h, :],
                        op0=ALU.mult, op1=ALU.subtract), reads=[IB, r1B, cmB], writes=[l2B])
                for h in range(4):
                    S.op("pe", lambda e, h=h: e.matmul(pBt[:, h, :], lhsT=onesf[:, :], rhs=r1[:, h, :], start=True, stop=True),
                         reads=[r1B, constB], writes=[pBtB])
                    S.op("pe", lambda e, h=h: e.matmul(pD[:, h, :], lhsT=onesf[:, :], rhs=r1[:, h, :], start=True, stop=False),
                         reads=[r1B, constB], writes=[pDB])
                    S.op("pe", lambda e, h=h: e.matmul(pD[:, h, :], lhsT=l2[:, h, :], rhs=onesf[:, :], start=False, stop=False),
                         reads=[l2B, constB], writes=[pDB])
                    S.op("pe", lambda e, h=h: e.matmul(pD[:, h, :], lhsT=identF, rhs=mB, start=False, stop=True),
                         reads=[cmB], writes=[pDB])
                    S.op("pe", lambda e, h=h: e.matmul(pS[:, h, :], lhsT=kT[sl][:, h, :], rhs=qT[sl][:, h, :], start=True, stop=True),
                         reads=[IB], writes=[pSB])
                if BSTEP < 3:
                    continue
                S.op("act", lambda e: e.activation(out=eBt[:, :, :], in_=pBt[:, :, :], func=AF.Exp), reads=[pBtB], writes=[eBtB])
                S.op("act", lambda e: e.activation(out=Dm[:, :, :], in_=pD[:, :, :], func=AF.Exp), reads=[pDB], writes=[DmB])
                S.op("dve", lambda e: e.tensor_copy(out=bL[:, :], in_=pBt[:, :, bcol]), reads=[pBtB], writes=[bLB])
                S.op("dve", lambda e: e.tensor_copy(out=wv[:, :], in_=Dm[:, :, bcol]), reads=[DmB], writes=[wvB])
                S.op("dve", lambda e: e.tensor_tensor(out=Pm[:, :, :], in0=pS[:, :, :], in1=Dm[:, :, :], op=ALU.mult),
                     reads=[pSB, DmB], writes=[PmB])
                S.op("dve", lambda e: e.tensor_tensor(out=qs[:, :, :], in0=qT[sl][:, :, :], in1=eBt[:, :, :], op=ALU.mult),
                     reads=[IB, eBtB], writes=[qsB])
                for h in range(4):
                    S.op("dve", lambda e, h=h: e.tensor_scalar(out=vw[:, h, :], in0=vtm[sl][:, h, :], scalar1=wv[:, h:h + 1],
                                                                scalar2=None, op0=ALU.mult),
                         reads=[IB, wvB], writes=[vwB])
                if BSTEP < 4:
                    continue
                hsl = n % 2
                for h in range(4):
                    pHh = pH[h % 2]; pHhB = pHB[h % 2]
                    S.op("pe", lambda e, h=h, pHh=pHh: e.matmul(pHh[:, 0:257], lhsT=qs[:, h, :], rhs=Cbf[:, h, :],
                                                                start=True, stop=False),
                         reads=[qsB, CbfB], writes=[pHhB])
                    S.op("pe", lambda e, h=h, pHh=pHh: e.matmul(pHh[:, 0:257], lhsT=Pm[:, h, :], rhs=vtm[sl][:, h, :],
                                                                start=False, stop=True),
                         reads=[PmB, IB], writes=[pHhB])
                    S.op("act", lambda e, h=h, pHh=pHh: e.activation(out=den[:, h:h + 1], in_=pHh[:, 256:257], func=AF.Abs),
                         reads=[pHhB], writes=[denB])
                    S.op("dve", lambda e, h=h: e.tensor_scalar(out=den[:, h:h + 1], in0=den[:, h:h + 1],
                                                               scalar1=1.0, scalar2=None, op0=ALU.max),
                         reads=[denB], writes=[denB])
                    S.op("dve", lambda e, h=h: e.reciprocal(out=den[:, 4 + h:5 + h], in_=den[:, h:h + 1]),
                         reads=[denB], writes=[denB])
                    S.op("act", lambda e, h=h, pHh=pHh: e.mul(out=ho[hsl][:, h * 256:(h + 1) * 256], in_=pHh[:, 0:256],
                                                              mul=den[:, 4 + h:5 + h]),
                         reads=[pHhB, denB], writes=[hoB[hsl]])
                S.op("sp", lambda e, ci=ci, hsl=hsl: e.dma_start(out=s_h[ci * 128:(ci + 1) * 128, :], in_=ho[hsl][:, :]),
                     reads=[hoB[hsl]], dma=hoB[hsl])
                if BSTEP < 5:
                    continue
                for h in range(4):
                    pCh = pC[h % 2]; pChB = pCB[h % 2]
                    S.op("pe", lambda e, h=h, pCh=pCh: e.matmul(pCh[:, 0:257], lhsT=ktm[sl][:, h * 128:(h + 1) * 128],
                                                                rhs=vw[:, h, :], start=True, stop=True),
                         reads=[IB, vwB], writes=[pChB])
                    S.op("dve", lambda e, h=h, pCh=pCh: e.scalar_tensor_tensor(
                        out=Cst[:, h, :], in0=Cst[:, h, :], scalar=eBt[:, h, bcol:bcol + 1], in1=pCh[:, 0:257],
                        op0=ALU.mult, op1=ALU.add), reads=[CstB, eBtB, pChB], writes=[CstB])
                S.op("dve", lambda e: e.tensor_copy(out=Cbf[:, :, :], in_=Cst[:, :, :]), reads=[CstB], writes=[CbfB])
        barrier()

    def build_mem_kv(st, li, nt_, kmT, kmTB, vm, vmB):
        with contextlib.ExitStack() as st2:
            Wkv = sb("M_W", [128, 8, 2048], BF16, st2)
            WkvB = Buf()
            load_w(c_w_kv[li], D, 2048, Wkv, WkvB)
            gme = sb("M_g", [128, D], F32, st2)
            gmeB = S.dbuf()
            load_gain(4 + li, gme, gmeB)
            mt = sb("M_x", [128, D], F32, st2)
            mtB = S.dbuf()
            mT = sb("M_xT", [128, 8, 128], BF16, st2)
            mTB = Buf()
            pp = ps("M_p", [128, 512], F32, st2)
            ppB = PBuf()
            S.op("dve", lambda e: e.memset(vm[:, :, :, :, 256:257], 1.0), writes=[vmB])
            for slot in range(2):
                for mc in range(2):
                    S.op("sp", lambda e, slot=slot, mc=mc: e.dma_start(out=mt[:, :], in_=mem_in[slot, mc * 128:(mc + 1) * 128, :]),
                         writes=[mtB], dma=mtB)
                    rmsnorm_T(nt_, mt[:, :], mtB, gme[:, :], gmeB, mT, mTB, 0)
                    for dc in range(8):
                        for c in range(8):
                            S.op("pe", lambda e, dc=dc, c=c: e.matmul(pp[:, 0:128], lhsT=Wkv[:, c, dc * 128:(dc + 1) * 128],
                                                                      rhs=mT[:, c, :], start=(c == 0), stop=(c == 7)),
                                 reads=[WkvB, mTB], writes=[ppB])
                        S.op("act", lambda e, dc=dc, slot=slot, mc=mc: e.copy(
                            out=kmT[:, dc, slot, mc * 128:(mc + 1) * 128], in_=pp[:, 0:128]), reads=[ppB], writes=[kmTB])
                    for hv in range(2):
                        for c in range(8):
                            S.op("pe", lambda e, hv=hv, c=c: e.matmul(pp[:, :], lhsT=mT[:, c, :],
                                                                      rhs=Wkv[:, c, 1024 + hv * 512:1024 + (hv + 1) * 512],
                                                                      start=(c == 0), stop=(c == 7)),
                                 reads=[WkvB, mTB], writes=[ppB])
                        S.op("act", lambda e, hv=hv, slot=slot, mc=mc: e.copy(
                            out=vm[:, slot, mc, hv * 2:hv * 2 + 2, 0:256], in_=pp[:, :].rearrange("p (h e) -> p h e", h=2)),
                            reads=[ppB], writes=[vmB])
            barrier()

    def cross_attn(cx, xs, xsB, slot):
        t = cx["nt"]
        rmsnorm_T(t, xs[:, :], xsB, cx["gc"][:, :], cx["gcB"], cx["xT"], cx["xTB"], 0)
        for half in range(2):
            p = cx["pq"]; pB = cx["pqB"]
            for dq in range(4):
                dc = half * 4 + dq
                for c in range(8):
                    S.op("pe", lambda e, dq=dq, dc=dc, c=c: e.matmul(p[:, dq * 128:(dq + 1) * 128],
                                                                     lhsT=cx["Wq"][:, c, dc * 128:(dc + 1) * 128],
                                                                     rhs=cx["xT"][:, c, :], start=(c == 0), stop=(c == 7)),
                         reads=[cx["WB"], cx["xTB"]], writes=[pB])
            S.op("act", lambda e, half=half: e.copy(out=cx["qT"][:, half * 4:(half + 1) * 4, :],
                                                    in_=p[:, :].rearrange("p (a t) -> p a t", a=4)),
                 reads=[pB], writes=[cx["qTB"]])
        for h in range(4):
            for mc in range(2):
                p = cx["psT"]; pB = cx["psTB"]
                for dd in range(2):
                    S.op("pe", lambda e, h=h, mc=mc, dd=dd: e.matmul(
                        p[:, mc, :], lhsT=cx["kmT"][:, h * 2 + dd, slot, mc * 128:(mc + 1) * 128],
                        rhs=cx["qT"][:, h * 2 + dd, :], start=(dd == 0), stop=(dd == 1)),
                        reads=[cx["kmTB"], cx["qTB"]], writes=[pB])
            S.op("act", lambda e: e.activation(out=cx["pT"][:, :, :], in_=cx["psT"][:, 0:2, :], func=AF.Exp, scale=1.0 / 16.0),
                 reads=[cx["psTB"]], writes=[cx["pTB"]])
            po = cx["po"]; poB = cx["poB"]
            for mc in range(2):
                S.op("pe", lambda e, h=h, mc=mc: e.matmul(po[:, 0:257], lhsT=cx["pT"][:, mc, :], rhs=cx["vm"][:, slot, mc, h, :],
                                                          start=(mc == 0), stop=(mc == 1)),
                     reads=[cx["pTB"], cx["vmB"]], writes=[poB])
            S.op("dve", lambda e: e.reciprocal(out=cx["rd"][:, 0:1], in_=po[:, 256:257]), reads=[poB], writes=[cx["rdB"]])
            S.op("act", lambda e, h=h: e.mul(out=cx["oc"][:, h * 256:(h + 1) * 256], in_=po[:, 0:256],
                                                    mul=cx["rd"][:, 0:1]), reads=[poB, cx["rdB"]], writes=[cx["ocB"]])
        transpose_to(t, cx["oc"], cx["ocB"], cx["ocT"], cx["ocTB"], 0)
        for hv in range(2):
            p = cx["pq"]; pB = cx["pqB"]
            for c in range(8):
                S.op("pe", lambda e, hv=hv, c=c: e.matmul(p[:, :], lhsT=cx["ocT"][:, c, :],
                                                          rhs=cx["Wo"][:, c, hv * 512:(hv + 1) * 512], start=(c == 0), stop=(c == 7)),
                     reads=[cx["WB"], cx["ocTB"]], writes=[pB])
            S.op("dve", lambda e, hv=hv: e.tensor_tensor(out=xs[:, hv * 512:(hv + 1) * 512], in0=p[:, :],
                                                         in1=xs[:, hv * 512:(hv + 1) * 512], op=ALU.add),
                 reads=[pB, xsB], writes=[xsB])

    def mk_cross(st, li, tag):
        cx = {}
        cx["nt"] = mk_norm_tiles(st, tag)
        cx["Wq"] = sb(tag + "Wq", [128, 8, D], BF16, st)
        cx["Wo"] = sb(tag + "Wo", [128, 8, D], BF16, st)
        cx["WB"] = Buf()
        load_w(c_w_q[li], D, D, cx["Wq"], cx["WB"])
        load_w(c_w_out[li], D, D, cx["Wo"], cx["WB"])
        cx["gc"] = sb(tag + "gc", [128, D], F32, st)
        cx["gcB"] = S.dbuf()
        load_gain(2 + li, cx["gc"], cx["gcB"])
        cx["kmT"] = sb(tag + "kmT", [128, 8, 2, 256], BF16, st)
        cx["kmTB"] = Buf()
        cx["vm"] = sb(tag + "vm", [128, 2, 2, 4, 257], BF16, st)
        cx["vmB"] = Buf()
        cx["xT"] = sb(tag + "xT", [128, 8, 128], BF16, st)
        cx["xTB"] = Buf()
        cx["qT"] = sb(tag + "qT", [128, 8, 128], BF16, st)
        cx["qTB"] = Buf()
        cx["pq"] = ps(tag + "pq", [128, 512], F32, st)
        cx["pqB"] = PBuf()
        cx["psT"] = ps(tag + "psT", [128, 4, 128], F32, st)
        cx["psTB"] = PBuf()
        cx["pT"] = sb(tag + "pT", [128, 2, 128], BF16, st)
        cx["pTB"] = Buf()
        cx["po"] = ps(tag + "po", [128, 512], F32, st)
        cx["poB"] = PBuf()
        cx["rd"] = sb(tag + "rd", [128, 2], F32, st)
        cx["rdB"] = Buf()
        cx["oc"] = sb(tag + "oc", [128, D], BF16, st)
        cx["ocB"] = Buf()
        cx["ocT"] = sb(tag + "ocT", [128, 8, 128], BF16, st)
        cx["ocTB"] = Buf()
        build_mem_kv(st, li, cx["nt"], cx["kmT"], cx["kmTB"], cx["vm"], cx["vmB"])
        return cx

    with contextlib.ExitStack() as st:
      if phase_on():
        cx = mk_cross(st, 0, "C1")
        Wo = sb("C1_Wo", [128, 8, D], BF16, st)
        WoB = Buf()
        load_w(a_w_out, D, D, Wo, WoB)
        ng = sb("C1_ng", [128, D], F32, st)
        ngB = S.dbuf()
        S.op("sp", lambda e: e.dma_start(out=ng[:, :], in_=a_norm_g.partition_broadcast(128)), writes=[ngB], dma=ngB)
        xs = [sb("C1_x%d" % i, [128, D], F32, st) for i in range(2)]
        xsB = [S.dbuf() for _ in range(2)]
        hf = [sb("C1_hf%d" % i, [128, D], F32, st) for i in range(2)]
        hb = [sb("C1_hb%d" % i, [128, D], F32, st) for i in range(2)]
        so = [sb("C1_so%d" % i, [128, D], F32, st) for i in range(2)]
        hB = [S.dbuf() for _ in range(2)]
        hs = sb("C1_hs", [128, D], F32, st)
        hsB = Buf()
        hq = sb("C1_hq", [128, 256], F32, st)
        hqB = Buf()
        sq = sb("C1_sq", [128, 16], F32, st)
        sqB = Buf()
        hn = sb("C1_hn", [128, D], BF16, st)
        hnB = Buf()
        hnT = sb("C1_hnT", [128, 8, 128], BF16, st)
        hnTB = Buf()
        pw = ps("C1_pw", [128, 512], F32, st)
        pwB = PBuf()

        def loadt(ti):
            i = ti % 2
            r0 = ti * 128
            S.op("sp", lambda e: e.dma_start(out=xs[i][:, :], in_=x_in[r0:r0 + 128, :]), writes=[xsB[i]], dma=xsB[i])
            S.op("sp", lambda e: e.dma_start(out=hf[i][:, :], in_=s_hf[r0:r0 + 128, :]), writes=[hB[i]], dma=hB[i])
            S.op("sp", lambda e: e.dma_start(out=hb[i][:, :], in_=s_hb[r0:r0 + 128, :]), writes=[hB[i]], dma=hB[i])
            S.op("sp", lambda e: e.dma_start(out=so[i][:, :], in_=s_so[r0:r0 + 128, :]), writes=[hB[i]], dma=hB[i])

        loadt(0)
        for ti in range(NT):
            i = ti % 2
            if ti + 1 < NT:
                loadt(ti + 1)
            S.op("dve", lambda e: e.tensor_tensor(out=hs[:, :], in0=hf[i][:, :], in1=hb[i][:, :], op=ALU.add),
                 reads=[hB[i]], writes=[hsB])
            for h in range(4):
                S.op("act", lambda e, h=h: e.activation(out=hq[:, :], in_=hs[:, h * 256:(h + 1) * 256], func=AF.Square,
                                                        accum_out=sq[:, h:h + 1]), reads=[hsB], writes=[hqB, sqB])
            S.op("dve", lambda e: e.tensor_scalar(out=sq[:, 4:8], in0=sq[:, 0:4], scalar1=1.0 / 256.0, scalar2=EPS,
                                                  op0=ALU.mult, op1=ALU.add), reads=[sqB], writes=[sqB])
            S.op("act", lambda e: e.sqrt(out=sq[:, 8:12], in_=sq[:, 4:8]), reads=[sqB], writes=[sqB])
            S.op("dve", lambda e: e.reciprocal(out=sq[:, 12:16], in_=sq[:, 8:12]), reads=[sqB], writes=[sqB])
            for h in range(4):
                S.op("dve", lambda e, h=h: e.scalar_tensor_tensor(
                    out=hs[:, h * 256:(h + 1) * 256], in0=hs[:, h * 256:(h + 1) * 256], scalar=sq[:, 12 + h:13 + h],
                    in1=ng[:, h * 256:(h + 1) * 256], op0=ALU.mult, op1=ALU.mult), reads=[hsB, sqB, ngB], writes=[hsB])
            S.op("pool", lambda e: e.tensor_tensor(out=hn[:, :], in0=hs[:, :], in1=so[i][:, :], op=ALU.mult),
                 reads=[hsB, hB[i]], writes=[hnB])
            transpose_to(cx["nt"], hn, hnB, hnT, hnTB, 0)
            for hv in range(2):
                for c in range(8):
                    S.op("pe", lambda e, hv=hv, c=c: e.matmul(pw[:, :], lhsT=hnT[:, c, :], rhs=Wo[:, c, hv * 512:(hv + 1) * 512],
                                                              start=(c == 0), stop=(c == 7)), reads=[WoB, hnTB], writes=[pwB])
                S.op("dve", lambda e, hv=hv: e.tensor_tensor(out=xs[i][:, hv * 512:(hv + 1) * 512], in0=pw[:, :],
                                                             in1=xs[i][:, hv * 512:(hv + 1) * 512], op=ALU.add),
                     reads=[pwB, xsB[i]], writes=[xsB[i]])
            cross_attn(cx, xs[i], xsB[i], 0 if ti < NT // 2 else 1)
            S.op("sp", lambda e, ti=ti: e.dma_start(out=s_x1[ti * 128:(ti + 1) * 128, :], in_=xs[i][:, :]),
                 reads=[xsB[i]], dma=xsB[i])
        barrier()

    def mlp_phase(li, src, dst, ntiles, final):
        with contextlib.ExitStack() as st:
            W1 = sb("F_W1", [128, 8, 4096], BF16, st)
            W2 = sb("F_W2", [128, 32, D], BF16, st)
            WB = Buf()
            load_w(f_w1[li], D, 4096, W1, WB)
            load_w(f_w2[li], 4096, D, W2, WB)
            gm_ = sb("F_g", [128, D], F32, st)
            gm_B = S.dbuf()
            load_gain(6 + li, gm_, gm_B)
            if final:
                gf = sb("F_gf", [128, D], F32, st)
                gfB = S.dbuf()
                load_gain(8, gf, gfB)
            nt_ = mk_norm_tiles(st, "F")
            GT = 2
            xs = [[sb("F_x%d_%d" % (i, j), [128, D], F32, st) for j in range(GT)] for i in range(2)]
            xsB = [[S.dbuf() for j in range(GT)] for i in range(2)]
            xT = sb("F_xT", [128, 8, GT * 128], BF16, st)
            xTB = Buf()
            hT = sb("F_hT", [128, 32, GT * 128], BF16, st)
            hTB = Buf()
            rl = [sb("F_rl%d" % i, [128, GT * 128], F32, st) for i in range(2)]
            rlB = [Buf() for _ in range(2)]
            p1 = [ps("F_p1%d" % i, [128, 512], F32, st) for i in range(2)]
            p1B = [PBuf() for _ in range(2)]
            p2 = [ps("F_p2%d" % i, [128, 512], F32, st) for i in range(2)]
            p2B = [PBuf() for _ in range(2)]
            ng = ntiles // GT

            def loadg(gi):
                i = gi % 2
                for j in range(GT):
                    r0 = (gi * GT + j) * 128
                    S.op("sp", lambda e, j=j, r0=r0: e.dma_start(out=xs[i][j][:, :], in_=src[r0:r0 + 128, :]),
                         writes=[xsB[i][j]], dma=xsB[i][j])

            loadg(0)
            for gi in range(ng):
                i = gi % 2
                if gi + 1 < ng:
                    loadg(gi + 1)
                for j in range(GT):
                    rmsnorm_T(nt_, xs[i][j][:, :], xsB[i][j], gm_[:, :], gm_B, xT, xTB, j * 128)
                for fc in range(32):
                    p = p1[fc % 2]; pB = p1B[fc % 2]
                    for c in range(8):
                        S.op("pe", lambda e, p=p, fc=fc, c=c: e.matmul(p[:, 0:GT * 128], lhsT=W1[:, c, fc * 128:(fc + 1) * 128],
                                                                       rhs=xT[:, c, :], start=(c == 0), stop=(c == 7)),
                             reads=[WB, xTB], writes=[pB])
                    r = rl[fc % 2]; rB = rlB[fc % 2]
                    S.op("act", lambda e, p=p, r=r: e.activation(out=r[:, :], in_=p[:, 0:GT * 128], func=AF.Relu), reads=[pB], writes=[rB])
                    S.op("pool" if fc % 2 else "dve", lambda e, r=r, fc=fc: e.tensor_tensor(out=hT[:, fc, :], in0=r[:, :], in1=r[:, :],
                                                                                          op=ALU.mult), reads=[rB], writes=[hTB])
                for j in range(GT):
                    for hv in range(2):
                        p = p2[hv]; pB = p2B[hv]
                        for fc in range(32):
                            S.op("pe", lambda e, p=p, fc=fc, hv=hv, j=j: e.matmul(
                                p[:, :], lhsT=hT[:, fc, j * 128:(j + 1) * 128], rhs=W2[:, fc, hv * 512:(hv + 1) * 512],
                                start=(fc == 0), stop=(fc == 31)), reads=[WB, hTB], writes=[pB])
                        S.op("dve", lambda e, p=p, hv=hv, j=j: e.tensor_tensor(
                            out=xs[i][j][:, hv * 512:(hv + 1) * 512], in0=p[:, :], in1=xs[i][j][:, hv * 512:(hv + 1) * 512],
                            op=ALU.add), reads=[pB, xsB[i][j]], writes=[xsB[i][j]])
                    if final:
                        t = nt_
                        xj = xs[i][j]
                        xjB = xsB[i][j]
                        S.op("act", lambda e, xj=xj: e.activation(out=t["junk"][:, :], in_=xj[:, :], func=AF.Square,
                                                                  accum_out=t["ss"][:, 0:1]), reads=[xjB], writes=[t["junkB"], t["ssB"]])
                        S.op("dve", lambda e: e.tensor_scalar(out=t["ss"][:, 1:2], in0=t["ss"][:, 0:1], scalar1=1.0 / D,
                                                              scalar2=EPS, op0=ALU.mult, op1=ALU.add), reads=[t["ssB"]], writes=[t["ssB"]])
                        S.op("act", lambda e: e.sqrt(out=t["ss"][:, 2:3], in_=t["ss"][:, 1:2]), reads=[t["ssB"]], writes=[t["ssB"]])
                        S.op("dve", lambda e: e.reciprocal(out=t["ss"][:, 3:4], in_=t["ss"][:, 2:3]), reads=[t["ssB"]], writes=[t["ssB"]])
                        S.op("dve", lambda e, xj=xj: e.scalar_tensor_tensor(out=xj[:, :], in0=xj[:, :], scalar=t["ss"][:, 3:4],
                                                                           in1=gf[:, :], op0=ALU.mult, op1=ALU.mult),
                             reads=[xjB, t["ssB"], gfB], writes=[xjB])
                    r0 = (gi * GT + j) * 128
                    S.op("sp", lambda e, j=j, r0=r0: e.dma_start(out=dst[r0:r0 + 128, :], in_=xs[i][j][:, :]),
                         reads=[xsB[i][j]], dma=xsB[i][j])
            barrier()

    if phase_on():
        mlp_phase(0, s_x1, s_x2, NT, False)

    with contextlib.ExitStack() as st:
      if phase_on():
        W = sb("Q_W", [128, 8, 3072], BF16, st)
        WB = Buf()
        load_w(b_w_qkv, D, 3072, W, WB)
        gm = sb("Q_g", [128, D], F32, st)
        gmB = S.dbuf()
        load_gain(1, gm, gmB)
        nt_ = mk_norm_tiles(st, "Q")
        xt = [sb("Q_x%d" % i, [128, D], F32, st) for i in range(2)]
        xtB = [S.dbuf() for _ in range(2)]
        xnT = sb("Q_xnT", [128, 8, 128], BF16, st)
        xnTB = Buf()
        pf = [ps("Q_pf%d" % i, [128, 512], F32, st) for i in range(2)]
        pfB = [PBuf() for _ in range(2)]
        oq = [sb("Q_oq%d" % i, [128, 8, 128], BF16, st) for i in range(2)]
        oqB = [S.dbuf() for _ in range(2)]
        okk = [sb("Q_ok%d" % i, [128, 8, 128], BF16, st) for i in range(2)]
        okB = [S.dbuf() for _ in range(2)]
        ovv = [sb("Q_ov%d" % i, [128, 8, 128], BF16, st) for i in range(2)]
        ovB = [S.dbuf() for _ in range(2)]

        def loadx3(ti):
            i = ti % 2
            S.op("sp", lambda e: e.dma_start(out=xt[i][:, :], in_=s_x2[ti * 128:(ti + 1) * 128, :]), writes=[xtB[i]], dma=xtB[i])

        loadx3(0)
        pi = 0
        for ti in range(NT):
            i = ti % 2
            if ti + 1 < NT:
                loadx3(ti + 1)
            rmsnorm_T(nt_, xt[i][:, :], xtB[i], gm[:, :], gmB, xnT, xnTB, 0)
            r0 = ti * 128
            for part in range(2):
                if part == 0 and ti >= NQT:
                    continue
                dstt = oq[i] if part == 0 else okk[i]
                dB = oqB[i] if part == 0 else okB[i]
                for half in range(2):
                    p = pf[pi % 2]; pB = pfB[pi % 2]; pi += 1
                    for hh in range(4):
                        col = part * 1024 + (half * 4 + hh) * 128
                        for c in range(8):
                            S.op("pe", lambda e, p=p, hh=hh, c=c, col=col: e.matmul(
                                p[:, hh * 128:(hh + 1) * 128], lhsT=W[:, c, col:col + 128], rhs=xnT[:, c, :],
                                start=(c == 0), stop=(c == 7)), reads=[WB, xnTB], writes=[pB])
                    S.op("act", lambda e, p=p, half=half, dstt=dstt: e.copy(
                        out=dstt[:, half * 4:(half + 1) * 4, :], in_=p[:, :].rearrange("p (h t) -> p h t", h=4)),
                        reads=[pB], writes=[dB])
                if part == 0:
                    S.op("sp", lambda e, r0=r0: e.dma_start(out=s_QT[:, :, r0:r0 + 128].rearrange("h p t -> p h t"),
                                                            in_=oq[i][:, :, :]), reads=[oqB[i]], dma=oqB[i])
                else:
                    S.op("sp", lambda e, r0=r0: e.dma_start(out=s_KT[:, :, r0:r0 + 128].rearrange("h p t -> p h t"),
                                                            in_=okk[i][:, :, :]), reads=[okB[i]], dma=okB[i])
            for hv in range(2):
                p = pf[pi % 2]; pB = pfB[pi % 2]; pi += 1
                for c in range(8):
                    S.op("pe", lambda e, p=p, c=c, hv=hv: e.matmul(p[:, :], lhsT=xnT[:, c, :],
                                                                   rhs=W[:, c, 2048 + hv * 512:2048 + (hv + 1) * 512],
                                                                   start=(c == 0), stop=(c == 7)), reads=[WB, xnTB], writes=[pB])
                S.op("dve", lambda e, p=p, hv=hv: e.tensor_copy(out=ovv[i][:, hv * 4:(hv + 1) * 4, :],
                                                                in_=p[:, :].rearrange("p (h e) -> p h e", h=4)),
                     reads=[pB], writes=[ovB[i]])
            S.op("sp", lambda e, ti=ti: e.dma_start(out=s_V[:, :, ti, :].rearrange("h p e -> p h e"), in_=ovv[i][:, :, :]),
                 reads=[ovB[i]], dma=ovB[i])
        barrier()

    with contextlib.ExitStack() as st:
      if phase_on():
        NQG = NQ // 512
        NKB = NT
        tabs = sb("T_tab", [32, 8], F32, st)
        tabB = S.dbuf()
        ohs = sb("T_oh", [32, 1408], F32, st)
        ohB = S.dbuf()
        S.op("sp", lambda e: e.dma_start(out=tabs[:, :], in_=rel_bias), writes=[tabB], dma=tabB)
        S.op("sp", lambda e: e.dma_start(out=ohs[:, :], in_=onehot), writes=[ohB], dma=ohB)
        fv = sb("T_fv", [8, 1408], F32, st)
        fvB = S.dbuf()
        pfv = ps("T_pfv", [128, 512], F32, st)
        pfvB = PBuf()
        for c0 in range(0, 1408, 512):
            n = min(512, 1408 - c0)
            S.op("pe", lambda e, c0=c0, n=n: e.matmul(pfv[0:8, 0:n], lhsT=tabs[:, :], rhs=ohs[:, c0:c0 + n], start=True, stop=True),
                 reads=[tabB, ohB], writes=[pfvB])
            S.op("dve", lambda e, c0=c0, n=n: e.tensor_copy(out=fv[:, c0:c0 + n], in_=pfv[0:8, 0:n]), reads=[pfvB], writes=[fvB])
        S.op("sp", lambda e: e.dma_start(out=s_fv, in_=fv[:, :]), reads=[fvB], dma=fvB)
        barrier()
        cfar = sb("T_cfar", [128, 8, 4], F32, st)
        cfarB = S.dbuf()
        for h in range(8):
            S.op("sp", lambda e, h=h: e.dma_start(out=cfar[:, h, 0:2], in_=s_fv[h:h + 1, 1280:1282].partition_broadcast(128)),
                 writes=[cfarB], dma=cfarB)
        for h in range(8):
            S.op("dve", lambda e, h=h: e.tensor_tensor(out=cfar[:, h, 2:3], in0=cfar[:, h, 1:2], in1=flg[:, 1:2], op=ALU.add),
                 reads=[cfarB, flgB], writes=[cfarB])
            S.op("dve", lambda e, h=h: e.tensor_copy(out=cfar[:, h, 3:4], in_=flg[:, 1:2]), reads=[cfarB, flgB], writes=[cfarB])
        zero_c = sb("T_zero", [128, 1], F32, st)
        S.op("dve", lambda e: e.memset(zero_c[:, :], 0.0), writes=[constB])
        lv = sb("T_lv", [1, 256], F32, st)
        lvB = S.dbuf()
        S.op("sp", lambda e: e.dma_start(out=lv[:, :], in_=b_lambda), writes=[lvB], dma=lvB)
        lt = sb("T_lt", [1, 136], F32, st)
        ltB = Buf()
        S.op("dve", lambda e: e.tensor_tensor(out=lt[:, 0:64], in0=lv[:, 0:64], in1=lv[:, 64:128], op=ALU.mult), reads=[lvB], writes=[ltB])
        S.op("dve", lambda e: e.tensor_tensor(out=lt[:, 64:128], in0=lv[:, 128:192], in1=lv[:, 192:256], op=ALU.mult), reads=[lvB, ltB], writes=[ltB])
        S.op("dve", lambda e: e.tensor_reduce(out=lt[:, 128:129], in_=lt[:, 0:64], axis=AX.X, op=ALU.add), reads=[ltB], writes=[ltB])
        S.op("dve", lambda e: e.tensor_reduce(out=lt[:, 129:130], in_=lt[:, 64:128], axis=AX.X, op=ALU.add), reads=[ltB], writes=[ltB])
        S.op("act", lambda e: e.activation(out=lt[:, 130:132], in_=lt[:, 128:130], func=AF.Exp), reads=[ltB], writes=[ltB])
        linit = 0.8 - 0.6 * math.exp(-0.3 * 1)
        S.op("dve", lambda e: e.tensor_tensor(out=lt[:, 132:133], in0=lt[:, 131:132], in1=lt[:, 130:131], op=ALU.subtract), reads=[ltB], writes=[ltB])
        S.op("dve", lambda e: e.tensor_scalar(out=lt[:, 133:134], in0=lt[:, 132:133], scalar1=-linit, scalar2=None, op0=ALU.add), reads=[ltB], writes=[ltB])
        pl = pfv
        plB = pfvB
        S.op("pe", lambda e: e.matmul(pl[:, 0:1], lhsT=onesf[0:1, :], rhs=lt[0:1, 133:134], start=True, stop=True),
             reads=[ltB, constB], writes=[plB])
        nlam = sb("T_nlam", [128, 1], F32, st)
        nlamB = Buf()
        S.op("dve", lambda e: e.tensor_copy(out=nlam[:, :], in_=pl[:, 0:1]), reads=[plB], writes=[nlamB])
        sg = sb("T_sg", [128, 1], F32, st)
        sgB = S.dbuf()
        S.op("sp", lambda e: e.dma_start(out=sg[:, :], in_=b_subln_g), writes=[sgB], dma=sgB)
        sg2 = sb("T_sg2", [128, 1], F32, st)
        S.op("dve", lambda e: e.tensor_scalar(out=sg2[:, :], in0=sg[:, :], scalar1=(1.0 - linit), scalar2=None, op0=ALU.mult),
             reads=[sgB], writes=[sgB])

        KT1 = sb("T_KT1", [128, L], BF16, st)
        KT2 = sb("T_KT2", [128, L], BF16, st)
        Vh = sb("T_V", [128, NT, 128], BF16, st)
        QT = sb("T_QT", [128, NQ], BF16, st)
        kvB = S.dbuf()
        hk = sb("T_hk", [128, 1152], F32, st)
        hkB = S.dbuf()
        strip = sb("T_strip", [128, 1152], F32, st)
        stripB = Buf()
        pSu = [ps("T_pSu%d" % u, [128, 2, 512], F32, st) for u in range(2)]
        pSuB = [PBuf(), PBuf()]
        pO1 = ps("T_pO1", [128, 512], F32, st)
        pO2 = ps("T_pO2", [128, 512], F32, st)
        pO = [pO1, pO2]
        pOB = [PBuf(), PBuf()]
        pZ2 = ps("T_pZ2", [128, 512], F32, st)
        pZ2B = PBuf()
        NP = 3
        Pu = [sb("T_P%d" % i, [128, 2, 512], BF16, st) for i in range(NP)]
        PuB = [Buf() for _ in range(NP)]
        accZ = [sb("T_accZ%d" % w, [128, 512], F32, st) for w in range(2)]
        accZB = [Buf(), Buf()]
        tmp2 = sb("T_tmp2", [128, 2, 512], F32, st)
        tmp2B = Buf()
        zs = sb("T_zs", [128, 512], F32, st)
        zsB = Buf()
        tmpb = [sb("T_tmp%d" % i, [128, 512], F32, st) for i in range(2)]
        tmpB = [Buf() for _ in range(2)]
        rz = sb("T_rz", [128, 512], F32, st)
        rzB = Buf()
        a1 = sb("T_a1", [128, 512], F32, st)
        a1B = Buf()
        a2 = sb("T_a2", [128, 512], F32, st)
        a2B = Buf()
        osq = sb("T_osq", [128, 512], F32, st)
        osqB = Buf()
        onb = [sb("T_on%d" % i, [128, 512], BF16, st) for i in range(2)]
        onB = [S.dbuf() for _ in range(2)]
        scale = 64.0 ** -0.5
        it = 0
        oi = 0
        for h in range(8):
            if h == 0:
                S.op("dve", lambda e: e.memset(KT1[64:128, :], 0.0), writes=[kvB])
                S.op("dve", lambda e: e.memset(KT2[0:64, :], 0.0), writes=[kvB])
            S.op("sp", lambda e, h=h: e.dma_start(out=KT1[0:64, :], in_=s_KT[h, 0:64, :]), writes=[kvB], dma=kvB)
            S.op("sp", lambda e, h=h: e.dma_start(out=KT2[64:128, :], in_=s_KT[h, 64:128, :]), writes=[kvB], dma=kvB)
            S.op("sp", lambda e, h=h: e.dma_start(out=Vh[:, :, :], in_=s_V[h, :, :, :]), writes=[kvB], dma=kvB)
            S.op("sp", lambda e, h=h: e.dma_start(out=QT[:, :], in_=s_QT[h, :, :]), writes=[kvB], dma=kvB)
            S.op("sp", lambda e, h=h: e.dma_start(
                out=hk[:, :], in_=bass.AP(tensor=s_fv.tensor, offset=s_fv[h:h + 1, 0:1].offset, ap=[[1, 128], [1, 1152]])),
                writes=[hkB], dma=hkB)
            for c0 in range(0, 1152, 384):
                S.op("pe", lambda e, c0=c0: e.matmul(pO1[:, 0:384], lhsT=Jf, rhs=hk[:, c0:c0 + 384], start=True, stop=True),
                     reads=[cmB, hkB], writes=[pOB[0]])
                S.op("dve", lambda e, c0=c0: e.tensor_copy(out=strip[:, c0:c0 + 384], in_=pO1[:, 0:384]),
                     reads=[pOB[0]], writes=[stripB])
            def bias_for(qg, kb):
                o = kb - 4 * qg
                near = (-1 <= o <= 4)
                if near:
                    bias_ap = cfar[:, h, 3:4] if kb >= NKB // 2 else zero_c[:, 0:1]
                elif kb < 4 * qg - 1:
                    bias_ap = cfar[:, h, 0:1]
                elif kb < NKB // 2:
                    bias_ap = cfar[:, h, 1:2]
                else:
                    bias_ap = cfar[:, h, 2:3]
                return near, bias_ap, o

            def emit_S(qg, kb, u):
                q0 = qg * 512
                for w in range(2):
                    dlo = 64 * w
                    S.op("pe", lambda e, w=w, dlo=dlo: e.matmul(
                        pSu[u][:, w, :], lhsT=(KT1 if w == 0 else KT2)[:, kb * 128:(kb + 1) * 128], rhs=QT[:, q0:q0 + 512],
                        start=True, stop=True), reads=[kvB], writes=[pSuB[u]])

            def emit_E(qg, kb, u, slot):
                near, bb, oo = bias_for(qg, kb)
                if near:
                    c0 = 512 - 128 * oo
                    for w in range(2):
                        S.op("dve", lambda e, w=w: e.scalar_tensor_tensor(
                            out=tmp2[:, w, :], in0=pSu[u][:, w, :], scalar=scale, in1=strip[:, c0:c0 + 512],
                            op0=ALU.mult, op1=ALU.add), reads=[pSuB[u], stripB], writes=[tmp2B])
                    S.op("act", lambda e: e.activation(out=Pu[slot][:, :, :], in_=tmp2[:, :, :], func=AF.Exp, bias=bb, scale=1.0),
                         reads=[tmp2B, cfarB, constB], writes=[PuB[slot]])
                else:
                    S.op("act", lambda e: e.activation(out=Pu[slot][:, :, :], in_=pSu[u][:, :, :], func=AF.Exp, bias=bb, scale=scale),
                         reads=[pSuB[u], cfarB], writes=[PuB[slot]])

            def emit_PV(kb, slot):
                for w in range(2):
                    S.op("pe", lambda e, w=w: e.matmul(
                        pO[w][:, :], lhsT=Vh[:, kb, :], rhs=Pu[slot][:, w, :], start=(kb == 0), stop=(kb == NKB - 1)),
                        reads=[kvB, PuB[slot]], writes=[pOB[w]])

            def emit_Z(kb, slot):
                if kb == 0:
                    S.op("dve", lambda e: e.tensor_copy(out=accZ[0][:, :], in_=Pu[slot][:, 0, :]), reads=[PuB[slot]], writes=[accZB[0]])
                else:
                    S.op("dve", lambda e: e.tensor_tensor(out=accZ[0][:, :], in0=accZ[0][:, :], in1=Pu[slot][:, 0, :], op=ALU.add),
                         reads=[PuB[slot], accZB[0]], writes=[accZB[0]])
                if kb % 2 == 0:
                    S.op("pe", lambda e: e.matmul(pZ2[:, :], lhsT=onesb[:, :], rhs=Pu[slot][:, 1, :],
                                                  start=(kb == 0), stop=(kb == NKB - 2)),
                         reads=[constB, PuB[slot]], writes=[pZ2B])
                elif kb == 1:
                    S.op("dve", lambda e: e.tensor_copy(out=accZ[1][:, :], in_=Pu[slot][:, 1, :]), reads=[PuB[slot]], writes=[accZB[1]])
                else:
                    S.op("dve", lambda e: e.tensor_tensor(out=accZ[1][:, :], in0=accZ[1][:, :], in1=Pu[slot][:, 1, :], op=ALU.add),
                         reads=[PuB[slot], accZB[1]], writes=[accZB[1]])

            flat = [(qg, kb) for qg in range(NQG) for kb in range(NKB)]
            emit_S(flat[0][0], flat[0][1], it % 2)
            for fi, (qg, kb) in enumerate(flat):
                q0 = qg * 512
                u = it % 2
                slot = it % NP
                it += 1
                nxt = flat[fi + 1] if fi + 1 < len(flat) else None
                emit_E(qg, kb, u, slot)
                if nxt:
                    emit_S(nxt[0], nxt[1], it % 2)
                emit_PV(kb, slot)
                emit_Z(kb, slot)
                if kb != NKB - 1:
                    continue
                S.op("pe", lambda e: e.matmul(pfv[:, :], lhsT=onesf[:, :], rhs=accZ[0][:, :], start=True, stop=True),
                     reads=[accZB[0], constB], writes=[pfvB])
                S.op("dve", lambda e: e.reciprocal(out=rz[:, :], in_=pfv[:, :]), reads=[pfvB], writes=[rzB])
                S.op("dve", lambda e: e.tensor_tensor(out=a1[:, :], in0=pO1[:, :], in1=rz[:, :], op=ALU.mult),
                     reads=[pOB[0], rzB], writes=[a1B])
                S.op("pe", lambda e: e.matmul(pfv[:, :], lhsT=onesf[:, :], rhs=accZ[1][:, :], start=True, stop=True),
                     reads=[accZB[1], constB], writes=[pfvB])
                S.op("dve", lambda e: e.tensor_copy(out=zs[:, :], in_=pfv[:, :]), reads=[pfvB, zsB], writes=[zsB])
                S.op("dve", lambda e: e.tensor_tensor(out=zs[:, :], in0=pZ2[:, :], in1=zs[:, :], op=ALU.add),
                     reads=[pZ2B, zsB], writes=[zsB])
                S.op("dve", lambda e: e.reciprocal(out=rz[:, :], in_=zs[:, :]), reads=[zsB, rzB], writes=[rzB])
                S.op("dve", lambda e: e.tensor_tensor(out=a2[:, :], in0=pO2[:, :], in1=rz[:, :], op=ALU.mult),
                     reads=[pOB[1], rzB], writes=[a2B])
                S.op("dve", lambda e: e.scalar_tensor_tensor(out=a1[:, :], in0=a2[:, :], scalar=nlam[:, 0:1], in1=a1[:, :],
                                                             op0=ALU.mult, op1=ALU.add), reads=[a1B, a2B, nlamB], writes=[a1B])
                S.op("act", lambda e: e.activation(out=osq[:, :], in_=a1[:, :], func=AF.Square), reads=[a1B], writes=[osqB])
                S.op("pe", lambda e: e.matmul(pfv[:, :], lhsT=onesf[:, :], rhs=osq[:, :], start=True, stop=True),
                     reads=[osqB, constB], writes=[pfvB])
                S.op("dve", lambda e: e.tensor_scalar(out=a2[:, :], in0=pfv[:, :], scalar1=1.0 / 128.0, scalar2=EPS,
                                                      op0=ALU.mult, op1=ALU.add), reads=[pfvB], writes=[a2B])
                S.op("act", lambda e: e.sqrt(out=a2[:, :], in_=a2[:, :]), reads=[a2B], writes=[a2B])
                S.op("dve", lambda e: e.reciprocal(out=a2[:, :], in_=a2[:, :]), reads=[a2B], writes=[a2B])
                ob = onb[oi % 2]; obB = onB[oi % 2]; oi += 1
                S.op("dve", lambda e, ob=ob: e.scalar_tensor_tensor(out=ob[:, :], in0=a1[:, :], scalar=sg2[:, 0:1], in1=a2[:, :],
                                                                   op0=ALU.mult, op1=ALU.mult), reads=[a1B, a2B, sgB], writes=[obB])
                S.op("sp", lambda e, ob=ob, h=h, q0=q0: e.dma_start(out=s_oT[h, :, q0:q0 + 512], in_=ob[:, :]),
                     reads=[obB], dma=obB)
        barrier()

    with contextlib.ExitStack() as st:
      if phase_on():
        cx = mk_cross(st, 1, "C4")
        Wo = sb("C4_Wo", [128, 8, D], BF16, st)
        WoB = Buf()
        load_w(b_w_out, D, D, Wo, WoB)
        xs = [sb("C4_x%d" % i, [128, D], F32, st) for i in range(2)]
        xsB = [S.dbuf() for _ in range(2)]
        oT = [sb("C4_oT%d" % i, [128, 8, 128], BF16, st) for i in range(2)]
        oTB = [S.dbuf() for _ in range(2)]
        pw = ps("C4_pw", [128, 512], F32, st)
        pwB = PBuf()

        def loadt4(ti):
            i = ti % 2
            r0 = ti * 128
            S.op("sp", lambda e: e.dma_start(out=xs[i][:, :], in_=s_x2[r0:r0 + 128, :]), writes=[xsB[i]], dma=xsB[i])
            S.op("sp", lambda e: e.dma_start(out=oT[i][:, :, :], in_=s_oT[:, :, r0:r0 + 128].rearrange("h p t -> p h t")),
                 writes=[oTB[i]], dma=oTB[i])

        loadt4(0)
        for ti in range(NQT):
            i = ti % 2
            if ti + 1 < NQT:
                loadt4(ti + 1)
            for hv in range(2):
                for c in range(8):
                    S.op("pe", lambda e, hv=hv, c=c: e.matmul(pw[:, :], lhsT=oT[i][:, c, :], rhs=Wo[:, c, hv * 512:(hv + 1) * 512],
                                                              start=(c == 0), stop=(c == 7)), reads=[WoB, oTB[i]], writes=[pwB])
                S.op("dve", lambda e, hv=hv: e.tensor_tensor(out=xs[i][:, hv * 512:(hv + 1) * 512], in0=pw[:, :],
                                                             in1=xs[i][:, hv * 512:(hv + 1) * 512], op=ALU.add),
                     reads=[pwB, xsB[i]], writes=[xsB[i]])
            cross_attn(cx, xs[i], xsB[i], 0)
            S.op("sp", lambda e, ti=ti: e.dma_start(out=s_x3[ti * 128:(ti + 1) * 128, :], in_=xs[i][:, :]),
                 reads=[xsB[i]], dma=xsB[i])
        barrier()

    if phase_on():
        mlp_phase(1, s_x3, y_out, NQT, True)

    S.emit()
    stack.close()
    return nc


def _rel_bucket_np(rp):
    import jax.numpy as jnp
    half = 16
    max_exact = 8
    rp = jnp.asarray(rp, dtype=jnp.int32)
    ret = (rp > 0).astype(jnp.int32) * half
    n = jnp.abs(rp)
    nf = jnp.maximum(n, 1).astype(jnp.float32)
    large = max_exact + (jnp.log(nf / max_exact) / math.log(128 / max_exact) * (half - max_exact)).astype(jnp.int32)
    large = jnp.minimum(large, half - 1)
    return np.asarray(ret + jnp.where(n < max_exact, n, large))


def _consts(reverse, two_seg):
    cm = np.zeros((128, 6, 128), np.float32)
    j = np.arange(128)
    cm[:, 0, :] = np.eye(128)
    cm[:, 1, :] = np.eye(128)[::-1]
    cm[:, 2, :] = (j[:, None] <= j[None, :])
    cm[:, 3, :] = (j[:, None] >= j[None, :])
    cm[:, 4, :] = np.where(j[:, None] <= j[None, :], 0.0, NEG)
    cm[:, 5, :] = np.where(j[:, None] >= j[None, :], 0.0, NEG)
    xs = np.arange(1280)
    delta = 639 - xs
    sgn = -1 if reverse else 1
    bk = _rel_bucket_np(sgn * delta)
    oh = np.zeros((32, 1408), np.float32)
    oh[bk, xs] = 1.0
    b_before = int(_rel_bucket_np(np.array([sgn * -100000]))[0])
    b_after = int(_rel_bucket_np(np.array([sgn * 100000]))[0])
    oh[b_before, 1280] = 1.0
    oh[b_after, 1281] = 1.0
    fl = np.zeros((128, 4), np.float32)
    fl[:, 0] = 0.0 if two_seg else 1.0
    fl[:, 1] = NEG if two_seg else 0.0
    return cm, oh, fl


_PROG = {}


def _get_prog(L):
    if L not in _PROG:
        _PROG[L] = build_program(L)
    return _PROG[L]


def make_core_inputs(stream_x, mems, reverse, two_seg, w):
    x = np.ascontiguousarray(stream_x[::-1] if reverse else stream_x, dtype=np.float32)
    mm = np.ascontiguousarray(mems[::-1] if reverse else mems, dtype=np.float32)
    wg = w["a_w_gate"][0]
    bgate = w["a_b_gate"][0]
    if reverse:
        perm = np.concatenate([np.arange(8, 16), np.arange(0, 8)])
        wg = wg[:, perm]
        bgate = bgate[perm]
    cm, oh, fl = _consts(reverse, two_seg)
    gains = np.stack([w["g_mix"][0], w["g_mix"][1], w["g_cross"][0], w["g_cross"][1], w["g_mem"][0], w["g_mem"][1],
                      w["g_mlp"][0], w["g_mlp"][1], w["g_final"]]).astype(np.float32)
    f32 = lambda a: np.ascontiguousarray(a, dtype=np.float32)
    return {
        "x": x, "mem": mm, "gains": gains,
        "a_w_in": f32(w["a_w_in"][0]), "a_w_gate": f32(wg), "a_b_gate": f32(bgate.reshape(1, 16)),
        "a_norm_g": f32(w["a_norm_g"][0].reshape(1, D)), "a_w_out": f32(w["a_w_out"][0]),
        "b_w_qkv": f32(w["b_w_qkv"][0]), "b_lambda": f32(w["b_lambda"][0].reshape(1, 256)),
        "b_subln_g": f32(w["b_subln_g"][0].reshape(128, 1)), "b_w_out": f32(w["b_w_out"][0]),
        "rel_bias": f32(w["rel_bias"]), "c_w_q": f32(w["c_w_q"]), "c_w_kv": f32(w["c_w_kv"]),
        "c_w_out": f32(w["c_w_out"]), "f_w1": f32(w["f_w1"]), "f_w2": f32(w["f_w2"]),
        "cmat": cm, "onehot": oh, "flags": fl,
    }


def kernel(x_prompt, x_sample, mem_prompt, mem_sample, **w):
    x_prompt = np.asarray(x_prompt); x_sample = np.asarray(x_sample)
    mem_prompt = np.asarray(mem_prompt); mem_sample = np.asarray(mem_sample)
    w = {k: np.asarray(v) for k, v in w.items()}
    L = x_prompt.shape[1]
    NQ = L // 2
    nc = _get_prog(L)
    in_maps = [None] * 8
    for b in range(2):
        mems = np.stack([mem_prompt[b], mem_prompt[b]])
        in_maps[b] = make_core_inputs(x_prompt[b], mems, False, False, w)
        in_maps[b + 4] = make_core_inputs(x_prompt[b], mems, True, False, w)
    for p in range(2):
        xs = np.concatenate([x_sample[2 * p], x_sample[2 * p + 1]], axis=0)
        mems = np.stack([mem_sample[2 * p], mem_sample[2 * p + 1]])
        in_maps[2 + p] = make_core_inputs(xs, mems, False, True, w)
        in_maps[6 + p] = make_core_inputs(xs, mems, True, True, w)
    res = run_bass_kernel_spmd(nc, in_maps, core_ids=list(range(8)))
    ys = [np.asarray(r["y"], dtype=np.float32) for r in res.results]
    y_prompt = np.stack([np.concatenate([ys[b], ys[b + 4][::-1]], axis=0) for b in range(2)])
    y_sample = np.stack([ys[2], ys[6][::-1], ys[3], ys[7][::-1]])
    return (y_prompt, y_sample)
```

```python
import contextlib
import os
import math
import numpy as np
import concourse.bass as bass
import concourse.mybir as mybir
from concourse.bass_utils import run_bass_kernel_spmd

F32 = mybir.dt.float32
BF16 = mybir.dt.bfloat16
ALU = mybir.AluOpType
AF = mybir.ActivationFunctionType
AX = mybir.AxisListType

D = 1024
EPS = 1e-6
NEG = -30000.0
STRICT = True


class Buf:
    __slots__ = ("w", "r", "dsem", "dcnt", "excl")

    def __init__(self, excl=False):
        self.w = None
        self.r = {}
        self.dsem = None
        self.dcnt = 0
        self.excl = excl


def PBuf():
    return Buf(excl=True)


class Rec:
    def __init__(self):
        self.call = None

    def __getattr__(self, name):
        def f(*a, **k):
            self.call = (name, a, k)
            return self
        return f


def _record(fn):
    r = Rec()
    fn(r)
    assert r.call is not None
    return r.call


def _play(eng, call):
    name, a, k = call
    return getattr(eng, name)(*a, **k)


class Sched:
    ENG = ["pe", "act", "dve", "pool", "sp"]

    def __init__(self, nc, stack):
        self.nc = nc
        self.stack = stack
        self.q = {e: [] for e in self.ENG}
        self.cnt = {e: 0 for e in self.ENG}
        self.sem = {e: stack.enter_context(nc.semaphore("s_" + e)) for e in self.ENG}
        self.bar = stack.enter_context(nc.semaphore("s_bar"))
        self.bar_cnt = 0
        self.waited = {}
        self.dbufs = []
        self.nsem = 0

    def dbuf(self):
        b = Buf()
        self.nsem += 1
        b.dsem = self.stack.enter_context(self.nc.semaphore("d%d" % self.nsem))
        self.dbufs.append(b)
        return b

    def _wait(self, e, ev):
        sem, val, key = ev
        if self.waited.get((e, key), 0) >= val:
            return
        self.waited[(e, key)] = val
        self.q[e].append(lambda eng, sem=sem, val=val: eng.wait_ge(sem, val))

    def op(self, e, fn, reads=(), writes=(), dma=None):
        deps = []
        xr = [b for b in reads if b.excl]
        if xr:
            reads = [b for b in reads if not b.excl]
            writes = list(writes) + xr
        for b in reads:
            if b.w is not None:
                deps.append(b.w)
        for b in writes:
            if b.w is not None:
                deps.append(b.w)
            deps.extend(b.r.values())
        for ev in deps:
            if ev[2] == e and (e == "pe" or not STRICT):
                continue
            self._wait(e, ev)
        if dma is None:
            self.cnt[e] += 1
            sem = self.sem[e]
            ev = (sem, self.cnt[e], e)
            call = _record(fn)
            self.q[e].append(lambda eng, call=call, sem=sem: _play(eng, call).then_inc(sem, 1))
        else:
            dma.dcnt += 16
            sem = dma.dsem
            ev = (sem, dma.dcnt, ("d", id(dma)))
            call = _record(fn)
            self.q[e].append(lambda eng, call=call, sem=sem: _play(eng, call).then_inc(sem, 16))
        for b in writes:
            b.w = ev
            b.r = {}
        for b in reads:
            b.r[ev[2]] = ev

    def barrier(self, dummy_src, dummy_dst):
        for e in ["pe", "act", "dve", "pool"]:
            if self.cnt[e]:
                self._wait("sp", (self.sem[e], self.cnt[e], e))
        for b in self.dbufs:
            if b.dcnt:
                self._wait("sp", (b.dsem, b.dcnt, ("d", id(b))))
        self.bar_cnt += 16
        bar = self.bar
        self.q["sp"].append(lambda eng: eng.dma_start(out=dummy_dst, in_=dummy_src).then_inc(bar, 16))
        for e in self.ENG:
            self._wait(e, (bar, self.bar_cnt, "bar"))

    def emit(self):
        nc = self.nc
        q = self.q
        with nc.Block() as block:
            @block.sync
            def _(eng):
                for f in q["sp"]:
                    f(eng)

            @block.tensor
            def _(eng):
                for f in q["pe"]:
                    f(eng)

            @block.scalar
            def _(eng):
                for f in q["act"]:
                    f(eng)

            @block.vector
            def _(eng):
                for f in q["dve"]:
                    f(eng)

            @block.gpsimd
            def _(eng):
                for f in q["pool"]:
                    f(eng)


def build_program(L, dbg=False):
    NT = L // 128
    NQ = L // 2
    NQT = NQ // 128
    nc = bass.Bass("TRN2", target_bir_lowering=False)
    stack = contextlib.ExitStack()

    def din(name, shape, dt=F32):
        return nc.dram_tensor(name, list(shape), dt, kind="ExternalInput").ap()

    def dscr(name, shape, dt=F32):
        return nc.dram_tensor(name, list(shape), dt).ap()

    x_in = din("x", [L, D])
    mem_in = din("mem", [2, 256, D])
    gains = din("gains", [9, D])
    a_w_in = din("a_w_in", [D, 3072])
    a_w_gate = din("a_w_gate", [D, 16])
    a_b_gate = din("a_b_gate", [1, 16])
    a_norm_g = din("a_norm_g", [1, D])
    a_w_out = din("a_w_out", [D, D])
    b_w_qkv = din("b_w_qkv", [D, 3072])
    b_lambda = din("b_lambda", [1, 256])
    b_subln_g = din("b_subln_g", [128, 1])
    b_w_out = din("b_w_out", [D, D])
    rel_bias = din("rel_bias", [32, 8])
    c_w_q = din("c_w_q", [2, D, D])
    c_w_kv = din("c_w_kv", [2, D, 2048])
    c_w_out = din("c_w_out", [2, D, D])
    f_w1 = din("f_w1", [2, D, 4096])
    f_w2 = din("f_w2", [2, 4096, D])
    cmat = din("cmat", [128, 6, 128])
    onehot = din("onehot", [32, 1408])
    flags = din("flags", [128, 4])
    y_out = nc.dram_tensor("y", [NQ, D], F32, kind="ExternalOutput").ap()

    s_qT = dscr("s_qT", [4, 128, L], BF16)
    s_kT = dscr("s_kT", [4, 128, L], BF16)
    s_k = dscr("s_k", [L, 512], BF16)
    s_v = dscr("s_v", [L, 4 * 257], BF16)
    s_so = dscr("s_so", [L, D], F32)
    s_g = dscr("s_g", [L, 16], F32)
    s_hf = dscr("s_hf", [L, D], F32)
    s_hb = dscr("s_hb", [L, D], F32)
    s_x1 = dscr("s_x1", [L, D], F32)
    s_x2 = dscr("s_x2", [L, D], F32)
    s_KT = dscr("s_KT", [8, 128, L], BF16)
    s_V = dscr("s_V", [8, 128, NT, 128], BF16)
    s_QT = dscr("s_QT", [8, 128, NQ], BF16)
    s_oT = dscr("s_oT", [8, 128, NQ], BF16)
    s_x3 = dscr("s_x3", [NQ, D], F32)
    s_fv = dscr("s_fv", [8, 1408], F32)
    s_dummy = dscr("s_dummy", [2, 16], F32)

    S = Sched(nc, stack)
    KSTOP = int(os.environ.get("KSTOP", "99"))
    BSTEP = int(os.environ.get("BSTEP", "99"))
    phase_no = [0]

    def phase_on():
        phase_no[0] += 1
        return phase_no[0] <= KSTOP

    uid = [0]

    def sb(name, shape, dt, st=None):
        uid[0] += 1
        return (st or stack).enter_context(nc.sbuf_tensor("%s_%d" % (name, uid[0]), list(shape), dt))

    def ps(name, shape, dt, st=None):
        uid[0] += 1
        return (st or stack).enter_context(nc.psum_tensor("%s_%d" % (name, uid[0]), list(shape), dt))

    cm = sb("cm", [128, 6, 128], F32)
    cmB = S.dbuf()
    identb = sb("identb", [128, 128], BF16)
    onesb = sb("onesb", [128, 128], BF16)
    onesf = sb("onesf", [128, 128], F32)
    mUb = sb("mUb", [128, 128], BF16)
    mLb = sb("mLb", [128, 128], BF16)
    flg = sb("flg", [128, 4], F32)
    flgB = S.dbuf()
    constB = Buf()
    S.op("sp", lambda e: e.dma_start(out=cm[:, :, :], in_=cmat), writes=[cmB], dma=cmB)
    S.op("sp", lambda e: e.dma_start(out=flg[:, :], in_=flags), writes=[flgB], dma=flgB)
    S.op("dve", lambda e: e.tensor_copy(out=identb[:, :], in_=cm[:, 0, :]), reads=[cmB], writes=[constB])
    S.op("dve", lambda e: e.tensor_copy(out=mUb[:, :], in_=cm[:, 4, :]), reads=[cmB], writes=[constB])
    S.op("dve", lambda e: e.tensor_copy(out=mLb[:, :], in_=cm[:, 5, :]), reads=[cmB], writes=[constB])
    S.op("dve", lambda e: e.memset(onesb[:, :], 1.0), writes=[constB])
    S.op("dve", lambda e: e.memset(onesf[:, :], 1.0), writes=[constB])
    identF = cm[:, 0, :]
    Jf = cm[:, 1, :]
    Uf = cm[:, 2, :]
    Lf = cm[:, 3, :]
    CB = [cmB, constB, flgB]

    stg = [sb("stg%d" % i, [128, 1024], F32) for i in range(2)]
    stgB = [S.dbuf() for _ in range(2)]
    stg_i = [0]
    cast_eng = ["act", "dve", "pool"]

    def load_w(w_ap, K, N, dst, dstB, n0dst=0):
        for k in range(K // 128):
            for n0 in range(0, N, 1024):
                n = min(1024, N - n0)
                i = stg_i[0] % 2
                ce = cast_eng[stg_i[0] % 3]
                stg_i[0] += 1
                S.op("sp", lambda e, i=i, k=k, n0=n0, n=n: e.dma_start(
                    out=stg[i][:, :n], in_=w_ap[k * 128:(k + 1) * 128, n0:n0 + n]),
                    writes=[stgB[i]], dma=stgB[i])
                if ce == "act":
                    S.op("act", lambda e, i=i, k=k, n0=n0, n=n: e.copy(
                        out=dst[:, k, n0dst + n0:n0dst + n0 + n], in_=stg[i][:, :n]),
                        reads=[stgB[i]], writes=[dstB])
                else:
                    S.op(ce, lambda e, i=i, k=k, n0=n0, n=n: e.tensor_copy(
                        out=dst[:, k, n0dst + n0:n0dst + n0 + n], in_=stg[i][:, :n]),
                        reads=[stgB[i]], writes=[dstB])

    def load_gain(idx, dst, dstB):
        S.op("sp", lambda e: e.dma_start(out=dst[:, :], in_=gains[idx:idx + 1, :].partition_broadcast(128)),
             writes=[dstB], dma=dstB)

    def mk_norm_tiles(st, tag):
        t = {}
        t["junk"] = sb("junk" + tag, [128, D], BF16, st)
        t["junkB"] = Buf()
        t["ss"] = sb("ss" + tag, [128, 4], F32, st)
        t["ssB"] = Buf()
        t["xn"] = sb("xn" + tag, [128, D], BF16, st)
        t["xnB"] = Buf()
        t["pt"] = ps("pt" + tag, [128, 8, 128], BF16, st)
        t["ptB"] = PBuf()
        return t

    def rmsnorm_T(t, x_sb, xB, g_sb, gB, xnT, xnTB, toff, width=D):
        nchunk = width // 128
        S.op("act", lambda e: e.activation(out=t["junk"][:, :width], in_=x_sb, func=AF.Square,
                                            accum_out=t["ss"][:, 0:1]),
             reads=[xB], writes=[t["junkB"], t["ssB"]])
        S.op("dve", lambda e: e.tensor_scalar(out=t["ss"][:, 1:2], in0=t["ss"][:, 0:1], scalar1=1.0 / width,
                                              scalar2=EPS, op0=ALU.mult, op1=ALU.add),
             reads=[t["ssB"]], writes=[t["ssB"]])
        S.op("act", lambda e: e.sqrt(out=t["ss"][:, 2:3], in_=t["ss"][:, 1:2]), reads=[t["ssB"]], writes=[t["ssB"]])
        S.op("dve", lambda e: e.reciprocal(out=t["ss"][:, 3:4], in_=t["ss"][:, 2:3]),
             reads=[t["ssB"]], writes=[t["ssB"]])
        S.op("dve", lambda e: e.scalar_tensor_tensor(out=t["xn"][:, :width], in0=x_sb, scalar=t["ss"][:, 3:4],
                                                     in1=g_sb, op0=ALU.mult, op1=ALU.mult),
             reads=[xB, t["ssB"], gB], writes=[t["xnB"]])
        for c in range(nchunk):
            S.op("pe", lambda e, c=c: e.transpose(out=t["pt"][:, c, :], in_=t["xn"][:, c * 128:(c + 1) * 128],
                                                  identity=identb[:, :]),
                 reads=[t["xnB"], constB], writes=[t["ptB"]])
        S.op("dve", lambda e: e.tensor_copy(out=xnT[:, :nchunk, toff:toff + 128], in_=t["pt"][:, :nchunk, :]),
             reads=[t["ptB"]], writes=[xnTB])

    def transpose_to(t, src_bf, srcB, dstT, dstTB, toff, nchunk=8):
        for c in range(nchunk):
            S.op("pe", lambda e, c=c: e.transpose(out=t["pt"][:, c, :], in_=src_bf[:, c * 128:(c + 1) * 128],
                                                  identity=identb[:, :]),
                 reads=[srcB, constB], writes=[t["ptB"]])
        S.op("dve", lambda e: e.tensor_copy(out=dstT[:, :nchunk, toff:toff + 128], in_=t["pt"][:, :nchunk, :]),
             reads=[t["ptB"]], writes=[dstTB])

    def barrier():
        S.barrier(s_dummy[0:1, :], s_dummy[1:2, :])

    with contextlib.ExitStack() as st:
      if phase_on():
        W = sb("A_W", [128, 8, 3072 + 512 + 16], BF16, st)
        WB = Buf()
        load_w(a_w_in, D, 3072, W, WB, 0)
        load_w(a_w_gate, D, 16, W, WB, 3072 + 512)
        gm = sb("A_g", [128, D], F32, st)
        gmB = S.dbuf()
        load_gain(0, gm, gmB)
        bg = sb("A_bg", [128, 16], F32, st)
        bgB = S.dbuf()
        S.op("sp", lambda e: e.dma_start(out=bg[:, :], in_=a_b_gate.partition_broadcast(128)), writes=[bgB], dma=bgB)
        nt_ = mk_norm_tiles(st, "A")
        xt = [sb("A_x%d" % i, [128, D], F32, st) for i in range(2)]
        xtB = [S.dbuf() for _ in range(2)]
        xnT = [sb("A_xnT%d" % i, [128, 8, 128], BF16, st) for i in range(2)]
        xnTB = [Buf() for _ in range(2)]
        pfm = [ps("A_pfm%d" % i, [128, 512], F32, st) for i in range(2)]
        pfmB = [PBuf() for _ in range(2)]
        ptm = [ps("A_ptm%d" % i, [128, 512], F32, st) for i in range(2)]
        ptmB = [PBuf() for _ in range(2)]
        oqk = [sb("A_oqk%d" % i, [128, 8, 128], BF16, st) for i in range(2)]
        oqkB = [S.dbuf() for _ in range(2)]
        ok = [sb("A_ok%d" % i, [128, 512], BF16, st) for i in range(2)]
        okB = [S.dbuf() for _ in range(2)]
        ov = [sb("A_ov%d" % i, [128, 4, 257], BF16, st) for i in range(2)]
        ovB = [S.dbuf() for _ in range(2)]
        oso = [sb("A_oso%d" % i, [128, D], F32, st) for i in range(2)]
        osoB = [S.dbuf() for _ in range(2)]
        og = [sb("A_og%d" % i, [128, 16], F32, st) for i in range(2)]
        ogB = [S.dbuf() for _ in range(2)]
        for i in range(2):
            S.op("dve", lambda e, i=i: e.memset(ov[i][:, :, 256:257], 1.0), writes=[ovB[i]])
        kscale = 128.0 ** -0.5

        def loadx(ti):
            i = ti % 2
            S.op("sp", lambda e: e.dma_start(out=xt[i][:, :], in_=x_in[ti * 128:(ti + 1) * 128, :]),
                 writes=[xtB[i]], dma=xtB[i])

        loadx(0)
        pi = 0
        for ti in range(NT):
            i = ti % 2
            if ti + 1 < NT:
                loadx(ti + 1)
            rmsnorm_T(nt_, xt[i][:, :], xtB[i], gm[:, :], gmB, xnT[i], xnTB[i], 0)
            for half in range(2):
                p = pfm[pi % 2]; pB = pfmB[pi % 2]; pi += 1
                for h in range(4):
                    col = half * 512 + h * 128
                    for c in range(8):
                        S.op("pe", lambda e, p=p, h=h, c=c, col=col: e.matmul(
                            p[:, h * 128:(h + 1) * 128], lhsT=W[:, c, col:col + 128], rhs=xnT[i][:, c, :],
                            start=(c == 0), stop=(c == 7)), reads=[WB, xnTB[i]], writes=[pB])
                S.op("act", lambda e, p=p, half=half: e.mul(
                    out=oqk[i][:, half * 4:(half + 1) * 4, :], in_=p[:, :].rearrange("p (h t) -> p h t", h=4),
                    mul=(1.0 if half == 0 else kscale)), reads=[pB], writes=[oqkB[i]])
            p = ptm[0]; pB = ptmB[0]
            for c in range(8):
                S.op("pe", lambda e, p=p, c=c: e.matmul(p[:, :], lhsT=xnT[i][:, c, :], rhs=W[:, c, 512:1024],
                                                        start=(c == 0), stop=(c == 7)),
                     reads=[WB, xnTB[i]], writes=[pB])
            S.op("act", lambda e, p=p: e.mul(out=ok[i][:, :], in_=p[:, :], mul=kscale),
                 reads=[pB], writes=[okB[i]])
            for hv in range(2):
                p = ptm[1]; pB = ptmB[1]
                for c in range(8):
                    S.op("pe", lambda e, p=p, c=c, hv=hv: e.matmul(
                        p[:, :], lhsT=xnT[i][:, c, :], rhs=W[:, c, 1024 + hv * 512:1024 + (hv + 1) * 512],
                        start=(c == 0), stop=(c == 7)), reads=[WB, xnTB[i]], writes=[pB])
                S.op("dve", lambda e, p=p, hv=hv: e.tensor_copy(
                    out=ov[i][:, hv * 2:(hv + 1) * 2, 0:256], in_=p[:, :].rearrange("p (h e) -> p h e", h=2)),
                    reads=[pB], writes=[ovB[i]])
            for ho in range(2):
                p = ptm[0]; pB = ptmB[0]
                for c in range(8):
                    S.op("pe", lambda e, p=p, c=c, ho=ho: e.matmul(
                        p[:, :], lhsT=xnT[i][:, c, :], rhs=W[:, c, 2048 + ho * 512:2048 + (ho + 1) * 512],
                        start=(c == 0), stop=(c == 7)), reads=[WB, xnTB[i]], writes=[pB])
                S.op("act", lambda e, p=p, ho=ho: e.activation(
                    out=oso[i][:, ho * 512:(ho + 1) * 512], in_=p[:, :], func=AF.Sigmoid),
                    reads=[pB], writes=[osoB[i]])
            p = ptm[1]; pB = ptmB[1]
            for c in range(8):
                S.op("pe", lambda e, p=p, c=c: e.matmul(p[:, 0:16], lhsT=xnT[i][:, c, :],
                                                        rhs=W[:, c, 3584:3600], start=(c == 0), stop=(c == 7)),
                     reads=[WB, xnTB[i]], writes=[pB])
            S.op("dve", lambda e, p=p: e.tensor_tensor(out=og[i][:, :], in0=p[:, 0:16], in1=bg[:, :], op=ALU.add),
                 reads=[pB, bgB], writes=[ogB[i]])
            r0 = ti * 128
            S.op("sp", lambda e, r0=r0: e.dma_start(
                out=s_qT[:, :, r0:r0 + 128].rearrange("h p t -> p h t"), in_=oqk[i][:, 0:4, :]),
                reads=[oqkB[i]], dma=oqkB[i])
            S.op("sp", lambda e, r0=r0: e.dma_start(
                out=s_kT[:, :, r0:r0 + 128].rearrange("h p t -> p h t"), in_=oqk[i][:, 4:8, :]),
                reads=[oqkB[i]], dma=oqkB[i])
            S.op("sp", lambda e, r0=r0: e.dma_start(out=s_k[r0:r0 + 128, :], in_=ok[i][:, :]),
                 reads=[okB[i]], dma=okB[i])
            S.op("sp", lambda e, r0=r0: e.dma_start(
                out=s_v[r0:r0 + 128, :], in_=ov[i][:, :, :].rearrange("p h e -> p (h e)")),
                reads=[ovB[i]], dma=ovB[i])
            S.op("sp", lambda e, r0=r0: e.dma_start(out=s_so[r0:r0 + 128, :], in_=oso[i][:, :]),
                 reads=[osoB[i]], dma=osoB[i])
            S.op("sp", lambda e, r0=r0: e.dma_start(out=s_g[r0:r0 + 128, :], in_=og[i][:, :]),
                 reads=[ogB[i]], dma=ogB[i])
        barrier()

    with contextlib.ExitStack() as st:
      if phase_on():
        NB = 3
        qT = [sb("B_qT%d" % i, [128, 4, 128], BF16, st) for i in range(NB)]
        kT = [sb("B_kT%d" % i, [128, 4, 128], BF16, st) for i in range(NB)]
        ktm = [sb("B_k%d" % i, [128, 512], BF16, st) for i in range(NB)]
        vtm = [sb("B_v%d" % i, [128, 4, 257], BF16, st) for i in range(NB)]
        gt = [sb("B_g%d" % i, [128, 16], F32, st) for i in range(NB)]
        inB = [S.dbuf() for _ in range(NB)]
        lg = sb("B_lg", [128, 16], F32, st)
        lgB = Buf()
        r1 = sb("B_r1", [128, 4, 128], F32, st)
        r1B = Buf()
        l2 = sb("B_l2", [128, 4, 128], F32, st)
        l2B = Buf()
        pBt = ps("B_pBt", [128, 4, 128], F32, st)
        pBtB = PBuf()
        pD = ps("B_pD", [128, 4, 128], F32, st)
        pDB = PBuf()
        pS = ps("B_pS", [128, 4, 128], F32, st)
        pSB = PBuf()
        pH = [ps("B_pH%d" % i, [128, 512], F32, st) for i in range(2)]
        pHB = [PBuf() for _ in range(2)]
        pC = [ps("B_pC%d" % i, [128, 512], F32, st) for i in range(2)]
        pCB = [PBuf() for _ in range(2)]
        eBt = sb("B_eBt", [128, 4, 128], F32, st)
        eBtB = Buf()
        bL = sb("B_bL", [128, 4], F32, st)
        bLB = Buf()
        wv = sb("B_w", [128, 4], F32, st)
        wvB = Buf()
        Dm = sb("B_D", [128, 4, 128], F32, st)
        DmB = Buf()
        Pm = sb("B_P", [128, 4, 128], BF16, st)
        PmB = Buf()
        qs = sb("B_qs", [128, 4, 128], BF16, st)
        qsB = Buf()
        vw = sb("B_vw", [128, 4, 257], BF16, st)
        vwB = Buf()
        Cst = sb("B_C", [128, 4, 257], F32, st)
        CstB = Buf()
        Cbf = sb("B_Cbf", [128, 4, 257], BF16, st)
        CbfB = Buf()
        den = sb("B_den", [128, 8], F32, st)
        denB = Buf()
        ho = [sb("B_ho%d" % i, [128, D], F32, st) for i in range(2)]
        hoB = [S.dbuf() for _ in range(2)]

        def loadc(ci, slot):
            r0 = ci * 128
            B_ = inB[slot]
            S.op("sp", lambda e: e.dma_start(out=qT[slot][:, :, :], in_=s_qT[:, :, r0:r0 + 128].rearrange("h p t -> p h t")),
                 writes=[B_], dma=B_)
            S.op("sp", lambda e: e.dma_start(out=kT[slot][:, :, :], in_=s_kT[:, :, r0:r0 + 128].rearrange("h p t -> p h t")),
                 writes=[B_], dma=B_)
            S.op("sp", lambda e: e.dma_start(out=ktm[slot][:, :], in_=s_k[r0:r0 + 128, :]), writes=[B_], dma=B_)
            S.op("sp", lambda e: e.dma_start(out=vtm[slot][:, :, :].rearrange("p h e -> p (h e)"), in_=s_v[r0:r0 + 128, :]),
                 writes=[B_], dma=B_)
            S.op("sp", lambda e: e.dma_start(out=gt[slot][:, :], in_=s_g[r0:r0 + 128, :]), writes=[B_], dma=B_)

        for direction in range(2):
            order = list(range(NT)) if direction == 0 else list(range(NT - 1, -1, -1))
            Mf = Uf if direction == 0 else Lf
            mB = cm[:, 4, :] if direction == 0 else cm[:, 5, :]
            gi0 = 0 if direction == 0 else 8
            s_h = s_hf if direction == 0 else s_hb
            bcol = 127 if direction == 0 else 0
            S.op("dve", lambda e: e.memset(Cst[:, :, :], 0.0), writes=[CstB])
            S.op("dve", lambda e: e.memset(Cbf[:, :, :], 0.0), writes=[CbfB])
            loadc(order[0], 0)
            if NT > 1:
                loadc(order[1], 1)
            for n, ci in enumerate(order):
                sl = n % NB
                if n + 2 < NT:
                    loadc(order[n + 2], (n + 2) % NB)
                IB = inB[sl]
                g_ = gt[sl]
                if n == NT // 2:
                    S.op("dve", lambda e: e.tensor_scalar(out=Cst[:, :, :], in0=Cst[:, :, :], scalar1=flg[:, 0:1],
                                                          scalar2=None, op0=ALU.mult), reads=[CstB, flgB], writes=[CstB])
                    S.op("dve", lambda e: e.tensor_copy(out=Cbf[:, :, :], in_=Cst[:, :, :]), reads=[CstB], writes=[CbfB])
                S.op("act", lambda e: e.activation(out=lg[:, 0:4], in_=g_[:, gi0 + 4:gi0 + 8], func=AF.Exp, scale=-1.0),
                     reads=[IB], writes=[lgB])
                S.op("act", lambda e: e.activation(out=lg[:, 4:8], in_=lg[:, 0:4], func=AF.Ln, bias=1.0),
                     reads=[lgB], writes=[lgB])
                S.op("dve", lambda e: e.tensor_scalar(out=lg[:, 8:12], in0=lg[:, 4:8], scalar1=-1.0, scalar2=None,
                                                      op0=ALU.mult), reads=[lgB], writes=[lgB])
                if BSTEP < 2:
                    continue
                for h in range(4):
                    S.op("dve", lambda e, h=h: e.tensor_scalar(out=r1[:, h, :], in0=Mf, scalar1=lg[:, 8 + h:9 + h],
                                                               scalar2=None, op0=ALU.mult),
                         reads=[lgB, cmB], writes=[r1B])
                    S.op("dve", lambda e, h=h: e.scalar_tensor_tensor(
                        out=l2[:, h, :], in0=identF, scalar=g_[:, gi0 + h:gi0 + h + 1], in1=r1[:, h, :],
                        op0=ALU.mult, op1=ALU.subtract), reads=[IB, r1B, cmB], writes=[l2B])
                for h in range(4):
                    S.op("pe", lambda e, h=h: e.matmul(pBt[:, h, :], lhsT=onesf[:, :], rhs=r1[:, h, :], start=True, stop=True),
                         reads=[r1B, constB], writes=[pBtB])
                    S.op("pe", lambda e, h=h: e.matmul(pD[:, h, :], lhsT=onesf[:, :], rhs=r1[:, h, :], start=True, stop=False),
                         reads=[r1B, constB], writes=[pDB])
                    S.op("pe", lambda e, h=h: e.matmul(pD[:, h, :], lhsT=l2[:, h, :], rhs=onesf[:, :], start=False, stop=False),
                         reads=[l2B, constB], writes=[pDB])
                    S.op("pe", lambda e, h=h: e.matmul(pD[:, h, :], lhsT=identF, rhs=mB, start=False, stop=True),
                         reads=[cmB], writes=[pDB])
                    S.op("pe", lambda e, h=h: e.matmul(pS[:, h, :], lhsT=kT[sl][:, h, :], rhs=qT[sl][:, h, :], start=True, stop=True),
                         reads=[IB], writes=[pSB])
                if BSTEP < 3:
                    continue
                S.op("act", lambda e: e.activation(out=eBt[:, :, :], in_=pBt[:, :, :], func=AF.Exp), reads=[pBtB], writes=[eBtB])
                S.op("act", lambda e: e.activation(out=Dm[:, :, :], in_=pD[:, :, :], func=AF.Exp), reads=[pDB], writes=[DmB])
                S.op("dve", lambda e: e.tensor_copy(out=bL[:, :], in_=pBt[:, :, bcol]), reads=[pBtB], writes=[bLB])
                S.op("dve", lambda e: e.tensor_copy(out=wv[:, :], in_=Dm[:, :, bcol]), reads=[DmB], writes=[wvB])
                S.op("dve", lambda e: e.tensor_tensor(out=Pm[:, :, :], in0=pS[:, :, :], in1=Dm[:, :, :], op=ALU.mult),
                     reads=[pSB, DmB], writes=[PmB])
                S.op("dve", lambda e: e.tensor_tensor(out=qs[:, :, :], in0=qT[sl][:, :, :], in1=eBt[:, :, :], op=ALU.mult),
                     reads=[IB, eBtB], writes=[qsB])
                for h in range(4):
                    S.op("dve", lambda e, h=h: e.tensor_scalar(out=vw[:, h, :], in0=vtm[sl][:, h, :], scalar1=wv[:, h:h + 1],
                                                                scalar2=None, op0=ALU.mult),
                         reads=[IB, wvB], writes=[vwB])
                if BSTEP < 4:
                    continue
                hsl = n % 2
                for h in range(4):
                    pHh = pH[h % 2]; pHhB = pHB[h % 2]
                    S.op("pe", lambda e, h=h, pHh=pHh: e.matmul(pHh[:, 0:257], lhsT=qs[:, h, :], rhs=Cbf[:, h, :],
                                                                start=True, stop=False),
                         reads=[qsB, CbfB], writes=[pHhB])
                    S.op("pe", lambda e, h=h, pHh=pHh: e.matmul(pHh[:, 0:257], lhsT=Pm[:, h, :], rhs=vtm[sl][:, h, :],
                                                                start=False, stop=True),
                         reads=[PmB, IB], writes=[pHhB])
                    S.op("act", lambda e, h=h, pHh=pHh: e.activation(out=den[:, h:h + 1], in_=pHh[:, 256:257], func=AF.Abs),
                         reads=[pHhB], writes=[denB])
                    S.op("dve", lambda e, h=h: e.tensor_scalar(out=den[:, h:h + 1], in0=den[:, h:h + 1],
                                                               scalar1=1.0, scalar2=None, op0=ALU.max),
                         reads=[denB], writes=[denB])
                    S.op("dve", lambda e, h=h: e.reciprocal(out=den[:, 4 + h:5 + h], in_=den[:, h:h + 1]),
                         reads=[denB], writes=[denB])
                    S.op("act", lambda e, h=h, pHh=pHh: e.mul(out=ho[hsl][:, h * 256:(h + 1) * 256], in_=pHh[:, 0:256],
                                                              mul=den[:, 4 + h:5 + h]),
                         reads=[pHhB, denB], writes=[hoB[hsl]])
                S.op("sp", lambda e, ci=ci, hsl=hsl: e.dma_start(out=s_h[ci * 128:(ci + 1) * 128, :], in_=ho[hsl][:, :]),
                     reads=[hoB[hsl]], dma=hoB[hsl])
                if BSTEP < 5:
                    continue
                for h in range(4):
                    pCh = pC[h % 2]; pChB = pCB[h % 2]
                    S.op("pe", lambda e, h=h, pCh=pCh: e.matmul(pCh[:, 0:257], lhsT=ktm[sl][:, h * 128:(h + 1) * 128],
                                                                rhs=vw[:, h, :], start=True, stop=True),
                         reads=[IB, vwB], writes=[pChB])
                    S.op("dve", lambda e, h=h, pCh=pCh: e.scalar_tensor_tensor(
                        out=Cst[:, h, :], in0=Cst[:, h, :], scalar=eBt[:, h, bcol:bcol + 1], in1=pCh[:, 0:257],
                        op0=ALU.mult, op1=ALU.add), reads=[CstB, eBtB, pChB], writes=[CstB])
                S.op("dve", lambda e: e.tensor_copy(out=Cbf[:, :, :], in_=Cst[:, :, :]), reads=[CstB], writes=[CbfB])
        barrier()

    def build_mem_kv(st, li, nt_, kmT, kmTB, vm, vmB):
        with contextlib.ExitStack() as st2:
            Wkv = sb("M_W", [128, 8, 2048], BF16, st2)
            WkvB = Buf()
            load_w(c_w_kv[li], D, 2048, Wkv, WkvB)
            gme = sb("M_g", [128, D], F32, st2)
            gmeB = S.dbuf()
            load_gain(4 + li, gme, gmeB)
            mt = sb("M_x", [128, D], F32, st2)
            mtB = S.dbuf()
            mT = sb("M_xT", [128, 8, 128], BF16, st2)
            mTB = Buf()
            pp = ps("M_p", [128, 512], F32, st2)
            ppB = PBuf()
            S.op("dve", lambda e: e.memset(vm[:, :, :, :, 256:257], 1.0), writes=[vmB])
            for slot in range(2):
                for mc in range(2):
                    S.op("sp", lambda e, slot=slot, mc=mc: e.dma_start(out=mt[:, :], in_=mem_in[slot, mc * 128:(mc + 1) * 128, :]),
                         writes=[mtB], dma=mtB)
                    rmsnorm_T(nt_, mt[:, :], mtB, gme[:, :], gmeB, mT, mTB, 0)
                    for dc in range(8):
                        for c in range(8):
                            S.op("pe", lambda e, dc=dc, c=c: e.matmul(pp[:, 0:128], lhsT=Wkv[:, c, dc * 128:(dc + 1) * 128],
                                                                      rhs=mT[:, c, :], start=(c == 0), stop=(c == 7)),
                                 reads=[WkvB, mTB], writes=[ppB])
                        S.op("act", lambda e, dc=dc, slot=slot, mc=mc: e.copy(
                            out=kmT[:, dc, slot, mc * 128:(mc + 1) * 128], in_=pp[:, 0:128]), reads=[ppB], writes=[kmTB])
                    for hv in range(2):
                        for c in range(8):
                            S.op("pe", lambda e, hv=hv, c=c: e.matmul(pp[:, :], lhsT=mT[:, c, :],
                                                                      rhs=Wkv[:, c, 1024 + hv * 512:1024 + (hv + 1) * 512],
                                                                      start=(c == 0), stop=(c == 7)),
                                 reads=[WkvB, mTB], writes=[ppB])
                        S.op("act", lambda e, hv=hv, slot=slot, mc=mc: e.copy(
                            out=vm[:, slot, mc, hv * 2:hv * 2 + 2, 0:256], in_=pp[:, :].rearrange("p (h e) -> p h e", h=2)),
                            reads=[ppB], writes=[vmB])
            barrier()

    def cross_attn(cx, xs, xsB, slot):
        t = cx["nt"]
        rmsnorm_T(t, xs[:, :], xsB, cx["gc"][:, :], cx["gcB"], cx["xT"], cx["xTB"], 0)
        for half in range(2):
            p = cx["pq"]; pB = cx["pqB"]
            for dq in range(4):
                dc = half * 4 + dq
                for c in range(8):
                    S.op("pe", lambda e, dq=dq, dc=dc, c=c: e.matmul(p[:, dq * 128:(dq + 1) * 128],
                                                                     lhsT=cx["Wq"][:, c, dc * 128:(dc + 1) * 128],
                                                                     rhs=cx["xT"][:, c, :], start=(c == 0), stop=(c == 7)),
                         reads=[cx["WB"], cx["xTB"]], writes=[pB])
            S.op("act", lambda e, half=half: e.copy(out=cx["qT"][:, half * 4:(half + 1) * 4, :],
                                                    in_=p[:, :].rearrange("p (a t) -> p a t", a=4)),
                 reads=[pB], writes=[cx["qTB"]])
        def _scores(h):
            p = cx["psT"][h % 2]; pB = cx["psTB"][h % 2]
            for mc in range(2):
                for dd in range(2):
                    S.op("pe", lambda e, mc=mc, dd=dd: e.matmul(
                        p[:, mc, :], lhsT=cx["kmT"][:, h * 2 + dd, slot, mc * 128:(mc + 1) * 128],
                        rhs=cx["qT"][:, h * 2 + dd, :], start=(dd == 0), stop=(dd == 1)),
                        reads=[cx["kmTB"], cx["qTB"]], writes=[pB])

        def _rest(h):
            k = h % 2
            pT = cx["pT"][k]; pTB = cx["pTB"][k]
            po = cx["po"][k]; poB = cx["poB"][k]
            rd = cx["rd"][k]; rdB = cx["rdB"][k]
            S.op("act", lambda e: e.activation(out=pT[:, :, :], in_=cx["psT"][k][:, 0:2, :], func=AF.Exp, scale=1.0 / 16.0),
                 reads=[cx["psTB"][k]], writes=[pTB])
            for mc in range(2):
                S.op("pe", lambda e, mc=mc: e.matmul(po[:, 0:257], lhsT=pT[:, mc, :], rhs=cx["vm"][:, slot, mc, h, :],
                                                     start=(mc == 0), stop=(mc == 1)),
                     reads=[pTB, cx["vmB"]], writes=[poB])
            S.op("dve", lambda e: e.reciprocal(out=rd[:, 0:1], in_=po[:, 256:257]), reads=[poB], writes=[rdB])
            S.op("act", lambda e: e.mul(out=cx["oc"][:, h * 256:(h + 1) * 256], in_=po[:, 0:256], mul=rd[:, 0:1]),
                 reads=[poB, rdB], writes=[cx["ocB"]])

        _scores(0)
        for h in range(4):
            if h + 1 < 4:
                _scores(h + 1)
            _rest(h)
        transpose_to(t, cx["oc"], cx["ocB"], cx["ocT"], cx["ocTB"], 0)
        for hv in range(2):
            p = cx["pq"]; pB = cx["pqB"]
            for c in range(8):
                S.op("pe", lambda e, hv=hv, c=c: e.matmul(p[:, :], lhsT=cx["ocT"][:, c, :],
                                                          rhs=cx["Wo"][:, c, hv * 512:(hv + 1) * 512], start=(c == 0), stop=(c == 7)),
                     reads=[cx["WB"], cx["ocTB"]], writes=[pB])
            S.op("dve", lambda e, hv=hv: e.tensor_tensor(out=xs[:, hv * 512:(hv + 1) * 512], in0=p[:, :],
                                                         in1=xs[:, hv * 512:(hv + 1) * 512], op=ALU.add),
                 reads=[pB, xsB], writes=[xsB])

    def mk_cross(st, li, tag):
        cx = {}
        cx["nt"] = mk_norm_tiles(st, tag)
        cx["Wq"] = sb(tag + "Wq", [128, 8, D], BF16, st)
        cx["Wo"] = sb(tag + "Wo", [128, 8, D], BF16, st)
        cx["WB"] = Buf()
        load_w(c_w_q[li], D, D, cx["Wq"], cx["WB"])
        load_w(c_w_out[li], D, D, cx["Wo"], cx["WB"])
        cx["gc"] = sb(tag + "gc", [128, D], F32, st)
        cx["gcB"] = S.dbuf()
        load_gain(2 + li, cx["gc"], cx["gcB"])
        cx["kmT"] = sb(tag + "kmT", [128, 8, 2, 256], BF16, st)
        cx["kmTB"] = Buf()
        cx["vm"] = sb(tag + "vm", [128, 2, 2, 4, 257], BF16, st)
        cx["vmB"] = Buf()
        cx["xT"] = sb(tag + "xT", [128, 8, 128], BF16, st)
        cx["xTB"] = Buf()
        cx["qT"] = sb(tag + "qT", [128, 8, 128], BF16, st)
        cx["qTB"] = Buf()
        cx["pq"] = ps(tag + "pq", [128, 512], F32, st)
        cx["pqB"] = PBuf()
        cx["psT"] = [ps(tag + "psT%d" % k, [128, 4, 128], F32, st) for k in range(2)]
        cx["psTB"] = [PBuf() for _ in range(2)]
        cx["pT"] = [sb(tag + "pT%d" % k, [128, 2, 128], BF16, st) for k in range(2)]
        cx["pTB"] = [Buf() for _ in range(2)]
        cx["po"] = [ps(tag + "po%d" % k, [128, 512], F32, st) for k in range(2)]
        cx["poB"] = [PBuf() for _ in range(2)]
        cx["rd"] = [sb(tag + "rd%d" % k, [128, 2], F32, st) for k in range(2)]
        cx["rdB"] = [Buf() for _ in range(2)]
        cx["oc"] = sb(tag + "oc", [128, D], BF16, st)
        cx["ocB"] = Buf()
        cx["ocT"] = sb(tag + "ocT", [128, 8, 128], BF16, st)
        cx["ocTB"] = Buf()
        build_mem_kv(st, li, cx["nt"], cx["kmT"], cx["kmTB"], cx["vm"], cx["vmB"])
        return cx

    with contextlib.ExitStack() as st:
      if phase_on():
        cx = mk_cross(st, 0, "C1")
        Wo = sb("C1_Wo", [128, 8, D], BF16, st)
        WoB = Buf()
        load_w(a_w_out, D, D, Wo, WoB)
        ng = sb("C1_ng", [128, D], F32, st)
        ngB = S.dbuf()
        S.op("sp", lambda e: e.dma_start(out=ng[:, :], in_=a_norm_g.partition_broadcast(128)), writes=[ngB], dma=ngB)
        xs = [sb("C1_x%d" % i, [128, D], F32, st) for i in range(2)]
        xsB = [S.dbuf() for _ in range(2)]
        hf = [sb("C1_hf%d" % i, [128, D], F32, st) for i in range(2)]
        hb = [sb("C1_hb%d" % i, [128, D], F32, st) for i in range(2)]
        so = [sb("C1_so%d" % i, [128, D], F32, st) for i in range(2)]
        hB = [S.dbuf() for _ in range(2)]
        hs = sb("C1_hs", [128, D], F32, st)
        hsB = Buf()
        hq = sb("C1_hq", [128, 256], F32, st)
        hqB = Buf()
        sq = sb("C1_sq", [128, 16], F32, st)
        sqB = Buf()
        hn = sb("C1_hn", [128, D], BF16, st)
        hnB = Buf()
        hnT = sb("C1_hnT", [128, 8, 128], BF16, st)
        hnTB = Buf()
        pw = ps("C1_pw", [128, 512], F32, st)
        pwB = PBuf()

        def loadt(ti):
            i = ti % 2
            r0 = ti * 128
            S.op("sp", lambda e: e.dma_start(out=xs[i][:, :], in_=x_in[r0:r0 + 128, :]), writes=[xsB[i]], dma=xsB[i])
            S.op("sp", lambda e: e.dma_start(out=hf[i][:, :], in_=s_hf[r0:r0 + 128, :]), writes=[hB[i]], dma=hB[i])
            S.op("sp", lambda e: e.dma_start(out=hb[i][:, :], in_=s_hb[r0:r0 + 128, :]), writes=[hB[i]], dma=hB[i])
            S.op("sp", lambda e: e.dma_start(out=so[i][:, :], in_=s_so[r0:r0 + 128, :]), writes=[hB[i]], dma=hB[i])

        loadt(0)
        for ti in range(NT):
            i = ti % 2
            if ti + 1 < NT:
                loadt(ti + 1)
            S.op("dve", lambda e: e.tensor_tensor(out=hs[:, :], in0=hf[i][:, :], in1=hb[i][:, :], op=ALU.add),
                 reads=[hB[i]], writes=[hsB])
            for h in range(4):
                S.op("act", lambda e, h=h: e.activation(out=hq[:, :], in_=hs[:, h * 256:(h + 1) * 256], func=AF.Square,
                                                        accum_out=sq[:, h:h + 1]), reads=[hsB], writes=[hqB, sqB])
            S.op("dve", lambda e: e.tensor_scalar(out=sq[:, 4:8], in0=sq[:, 0:4], scalar1=1.0 / 256.0, scalar2=EPS,
                                                  op0=ALU.mult, op1=ALU.add), reads=[sqB], writes=[sqB])
            S.op("act", lambda e: e.sqrt(out=sq[:, 8:12], in_=sq[:, 4:8]), reads=[sqB], writes=[sqB])
            S.op("dve", lambda e: e.reciprocal(out=sq[:, 12:16], in_=sq[:, 8:12]), reads=[sqB], writes=[sqB])
            for h in range(4):
                S.op("dve", lambda e, h=h: e.scalar_tensor_tensor(
                    out=hs[:, h * 256:(h + 1) * 256], in0=hs[:, h * 256:(h + 1) * 256], scalar=sq[:, 12 + h:13 + h],
                    in1=ng[:, h * 256:(h + 1) * 256], op0=ALU.mult, op1=ALU.mult), reads=[hsB, sqB, ngB], writes=[hsB])
            S.op("pool", lambda e: e.tensor_tensor(out=hn[:, :], in0=hs[:, :], in1=so[i][:, :], op=ALU.mult),
                 reads=[hsB, hB[i]], writes=[hnB])
            transpose_to(cx["nt"], hn, hnB, hnT, hnTB, 0)
            for hv in range(2):
                for c in range(8):
                    S.op("pe", lambda e, hv=hv, c=c: e.matmul(pw[:, :], lhsT=hnT[:, c, :], rhs=Wo[:, c, hv * 512:(hv + 1) * 512],
                                                              start=(c == 0), stop=(c == 7)), reads=[WoB, hnTB], writes=[pwB])
                S.op("dve", lambda e, hv=hv: e.tensor_tensor(out=xs[i][:, hv * 512:(hv + 1) * 512], in0=pw[:, :],
                                                             in1=xs[i][:, hv * 512:(hv + 1) * 512], op=ALU.add),
                     reads=[pwB, xsB[i]], writes=[xsB[i]])
            cross_attn(cx, xs[i], xsB[i], 0 if ti < NT // 2 else 1)
            S.op("sp", lambda e, ti=ti: e.dma_start(out=s_x1[ti * 128:(ti + 1) * 128, :], in_=xs[i][:, :]),
                 reads=[xsB[i]], dma=xsB[i])
        barrier()

    def mlp_phase(li, src, dst, ntiles, final):
        with contextlib.ExitStack() as st:
            W1 = sb("F_W1", [128, 8, 4096], BF16, st)
            W2 = sb("F_W2", [128, 32, D], BF16, st)
            WB = Buf()
            load_w(f_w1[li], D, 4096, W1, WB)
            load_w(f_w2[li], 4096, D, W2, WB)
            gm_ = sb("F_g", [128, D], F32, st)
            gm_B = S.dbuf()
            load_gain(6 + li, gm_, gm_B)
            if final:
                gf = sb("F_gf", [128, D], F32, st)
                gfB = S.dbuf()
                load_gain(8, gf, gfB)
            nt_ = mk_norm_tiles(st, "F")
            GT = 2
            xs = [[sb("F_x%d_%d" % (i, j), [128, D], F32, st) for j in range(GT)] for i in range(2)]
            xsB = [[S.dbuf() for j in range(GT)] for i in range(2)]
            xT = sb("F_xT", [128, 8, GT * 128], BF16, st)
            xTB = Buf()
            hT = sb("F_hT", [128, 32, GT * 128], BF16, st)
            hTB = Buf()
            rl = [sb("F_rl%d" % i, [128, GT * 128], F32, st) for i in range(2)]
            rlB = [Buf() for _ in range(2)]
            p1 = [ps("F_p1%d" % i, [128, 512], F32, st) for i in range(2)]
            p1B = [PBuf() for _ in range(2)]
            p2 = [ps("F_p2%d" % i, [128, 512], F32, st) for i in range(2)]
            p2B = [PBuf() for _ in range(2)]
            ng = ntiles // GT

            def loadg(gi):
                i = gi % 2
                for j in range(GT):
                    r0 = (gi * GT + j) * 128
                    S.op("sp", lambda e, j=j, r0=r0: e.dma_start(out=xs[i][j][:, :], in_=src[r0:r0 + 128, :]),
                         writes=[xsB[i][j]], dma=xsB[i][j])

            loadg(0)
            for gi in range(ng):
                i = gi % 2
                if gi + 1 < ng:
                    loadg(gi + 1)
                for j in range(GT):
                    rmsnorm_T(nt_, xs[i][j][:, :], xsB[i][j], gm_[:, :], gm_B, xT, xTB, j * 128)
                for fc in range(32):
                    p = p1[fc % 2]; pB = p1B[fc % 2]
                    for c in range(8):
                        S.op("pe", lambda e, p=p, fc=fc, c=c: e.matmul(p[:, 0:GT * 128], lhsT=W1[:, c, fc * 128:(fc + 1) * 128],
                                                                       rhs=xT[:, c, :], start=(c == 0), stop=(c == 7)),
                             reads=[WB, xTB], writes=[pB])
                    r = rl[fc % 2]; rB = rlB[fc % 2]
                    S.op("act", lambda e, p=p, r=r: e.activation(out=r[:, :], in_=p[:, 0:GT * 128], func=AF.Relu), reads=[pB], writes=[rB])
                    S.op("pool" if fc % 2 else "dve", lambda e, r=r, fc=fc: e.tensor_tensor(out=hT[:, fc, :], in0=r[:, :], in1=r[:, :],
                                                                                          op=ALU.mult), reads=[rB], writes=[hTB])
                for j in range(GT):
                    for hv in range(2):
                        p = p2[hv]; pB = p2B[hv]
                        for fc in range(32):
                            S.op("pe", lambda e, p=p, fc=fc, hv=hv, j=j: e.matmul(
                                p[:, :], lhsT=hT[:, fc, j * 128:(j + 1) * 128], rhs=W2[:, fc, hv * 512:(hv + 1) * 512],
                                start=(fc == 0), stop=(fc == 31)), reads=[WB, hTB], writes=[pB])
                        S.op("dve", lambda e, p=p, hv=hv, j=j: e.tensor_tensor(
                            out=xs[i][j][:, hv * 512:(hv + 1) * 512], in0=p[:, :], in1=xs[i][j][:, hv * 512:(hv + 1) * 512],
                            op=ALU.add), reads=[pB, xsB[i][j]], writes=[xsB[i][j]])
                    if final:
                        t = nt_
                        xj = xs[i][j]
                        xjB = xsB[i][j]
                        S.op("act", lambda e, xj=xj: e.activation(out=t["junk"][:, :], in_=xj[:, :], func=AF.Square,
                                                                  accum_out=t["ss"][:, 0:1]), reads=[xjB], writes=[t["junkB"], t["ssB"]])
                        S.op("dve", lambda e: e.tensor_scalar(out=t["ss"][:, 1:2], in0=t["ss"][:, 0:1], scalar1=1.0 / D,
                                                              scalar2=EPS, op0=ALU.mult, op1=ALU.add), reads=[t["ssB"]], writes=[t["ssB"]])
                        S.op("act", lambda e: e.sqrt(out=t["ss"][:, 2:3], in_=t["ss"][:, 1:2]), reads=[t["ssB"]], writes=[t["ssB"]])
                        S.op("dve", lambda e: e.reciprocal(out=t["ss"][:, 3:4], in_=t["ss"][:, 2:3]), reads=[t["ssB"]], writes=[t["ssB"]])
                        S.op("dve", lambda e, xj=xj: e.scalar_tensor_tensor(out=xj[:, :], in0=xj[:, :], scalar=t["ss"][:, 3:4],
                                                                           in1=gf[:, :], op0=ALU.mult, op1=ALU.mult),
                             reads=[xjB, t["ssB"], gfB], writes=[xjB])
                    r0 = (gi * GT + j) * 128
                    S.op("sp", lambda e, j=j, r0=r0: e.dma_start(out=dst[r0:r0 + 128, :], in_=xs[i][j][:, :]),
                         reads=[xsB[i][j]], dma=xsB[i][j])
            barrier()

    if phase_on():
        mlp_phase(0, s_x1, s_x2, NT, False)

    with contextlib.ExitStack() as st:
      if phase_on():
        W = sb("Q_W", [128, 8, 3072], BF16, st)
        WB = Buf()
        load_w(b_w_qkv, D, 3072, W, WB)
        gm = sb("Q_g", [128, D], F32, st)
        gmB = S.dbuf()
        load_gain(1, gm, gmB)
        nt_ = mk_norm_tiles(st, "Q")
        xt = [sb("Q_x%d" % i, [128, D], F32, st) for i in range(2)]
        xtB = [S.dbuf() for _ in range(2)]
        xnT = sb("Q_xnT", [128, 8, 128], BF16, st)
        xnTB = Buf()
        pf = [ps("Q_pf%d" % i, [128, 512], F32, st) for i in range(2)]
        pfB = [PBuf() for _ in range(2)]
        oq = [sb("Q_oq%d" % i, [128, 8, 128], BF16, st) for i in range(2)]
        oqB = [S.dbuf() for _ in range(2)]
        okk = [sb("Q_ok%d" % i, [128, 8, 128], BF16, st) for i in range(2)]
        okB = [S.dbuf() for _ in range(2)]
        ovv = [sb("Q_ov%d" % i, [128, 8, 128], BF16, st) for i in range(2)]
        ovB = [S.dbuf() for _ in range(2)]

        def loadx3(ti):
            i = ti % 2
            S.op("sp", lambda e: e.dma_start(out=xt[i][:, :], in_=s_x2[ti * 128:(ti + 1) * 128, :]), writes=[xtB[i]], dma=xtB[i])

        loadx3(0)
        pi = 0
        for ti in range(NT):
            i = ti % 2
            if ti + 1 < NT:
                loadx3(ti + 1)
            rmsnorm_T(nt_, xt[i][:, :], xtB[i], gm[:, :], gmB, xnT, xnTB, 0)
            r0 = ti * 128
            for part in range(2):
                if part == 0 and ti >= NQT:
                    continue
                dstt = oq[i] if part == 0 else okk[i]
                dB = oqB[i] if part == 0 else okB[i]
                for half in range(2):
                    p = pf[pi % 2]; pB = pfB[pi % 2]; pi += 1
                    for hh in range(4):
                        col = part * 1024 + (half * 4 + hh) * 128
                        for c in range(8):
                            S.op("pe", lambda e, p=p, hh=hh, c=c, col=col: e.matmul(
                                p[:, hh * 128:(hh + 1) * 128], lhsT=W[:, c, col:col + 128], rhs=xnT[:, c, :],
                                start=(c == 0), stop=(c == 7)), reads=[WB, xnTB], writes=[pB])
                    S.op("act", lambda e, p=p, half=half, dstt=dstt: e.copy(
                        out=dstt[:, half * 4:(half + 1) * 4, :], in_=p[:, :].rearrange("p (h t) -> p h t", h=4)),
                        reads=[pB], writes=[dB])
                if part == 0:
                    S.op("sp", lambda e, r0=r0: e.dma_start(out=s_QT[:, :, r0:r0 + 128].rearrange("h p t -> p h t"),
                                                            in_=oq[i][:, :, :]), reads=[oqB[i]], dma=oqB[i])
                else:
                    S.op("sp", lambda e, r0=r0: e.dma_start(out=s_KT[:, :, r0:r0 + 128].rearrange("h p t -> p h t"),
                                                            in_=okk[i][:, :, :]), reads=[okB[i]], dma=okB[i])
            for hv in range(2):
                p = pf[pi % 2]; pB = pfB[pi % 2]; pi += 1
                for c in range(8):
                    S.op("pe", lambda e, p=p, c=c, hv=hv: e.matmul(p[:, :], lhsT=xnT[:, c, :],
                                                                   rhs=W[:, c, 2048 + hv * 512:2048 + (hv + 1) * 512],
                                                                   start=(c == 0), stop=(c == 7)), reads=[WB, xnTB], writes=[pB])
                S.op("dve", lambda e, p=p, hv=hv: e.tensor_copy(out=ovv[i][:, hv * 4:(hv + 1) * 4, :],
                                                                in_=p[:, :].rearrange("p (h e) -> p h e", h=4)),
                     reads=[pB], writes=[ovB[i]])
            S.op("sp", lambda e, ti=ti: e.dma_start(out=s_V[:, :, ti, :].rearrange("h p e -> p h e"), in_=ovv[i][:, :, :]),
                 reads=[ovB[i]], dma=ovB[i])
        barrier()

    with contextlib.ExitStack() as st:
      if phase_on():
        NQG = NQ // 512
        NKB = NT
        tabs = sb("T_tab", [32, 8], F32, st)
        tabB = S.dbuf()
        ohs = sb("T_oh", [32, 1408], F32, st)
        ohB = S.dbuf()
        S.op("sp", lambda e: e.dma_start(out=tabs[:, :], in_=rel_bias), writes=[tabB], dma=tabB)
        S.op("sp", lambda e: e.dma_start(out=ohs[:, :], in_=onehot), writes=[ohB], dma=ohB)
        fv = sb("T_fv", [8, 1408], F32, st)
        fvB = S.dbuf()
        pfv = ps("T_pfv", [128, 512], F32, st)
        pfvB = PBuf()
        for c0 in range(0, 1408, 512):
            n = min(512, 1408 - c0)
            S.op("pe", lambda e, c0=c0, n=n: e.matmul(pfv[0:8, 0:n], lhsT=tabs[:, :], rhs=ohs[:, c0:c0 + n], start=True, stop=True),
                 reads=[tabB, ohB], writes=[pfvB])
            S.op("dve", lambda e, c0=c0, n=n: e.tensor_copy(out=fv[:, c0:c0 + n], in_=pfv[0:8, 0:n]), reads=[pfvB], writes=[fvB])
        S.op("sp", lambda e: e.dma_start(out=s_fv, in_=fv[:, :]), reads=[fvB], dma=fvB)
        barrier()
        cfar = sb("T_cfar", [128, 8, 4], F32, st)
        cfarB = S.dbuf()
        for h in range(8):
            S.op("sp", lambda e, h=h: e.dma_start(out=cfar[:, h, 0:2], in_=s_fv[h:h + 1, 1280:1282].partition_broadcast(128)),
                 writes=[cfarB], dma=cfarB)
        for h in range(8):
            S.op("dve", lambda e, h=h: e.tensor_tensor(out=cfar[:, h, 2:3], in0=cfar[:, h, 1:2], in1=flg[:, 1:2], op=ALU.add),
                 reads=[cfarB, flgB], writes=[cfarB])
            S.op("dve", lambda e, h=h: e.tensor_copy(out=cfar[:, h, 3:4], in_=flg[:, 1:2]), reads=[cfarB, flgB], writes=[cfarB])
        zero_c = sb("T_zero", [128, 1], F32, st)
        S.op("dve", lambda e: e.memset(zero_c[:, :], 0.0), writes=[constB])
        lv = sb("T_lv", [1, 256], F32, st)
        lvB = S.dbuf()
        S.op("sp", lambda e: e.dma_start(out=lv[:, :], in_=b_lambda), writes=[lvB], dma=lvB)
        lt = sb("T_lt", [1, 136], F32, st)
        ltB = Buf()
        S.op("dve", lambda e: e.tensor_tensor(out=lt[:, 0:64], in0=lv[:, 0:64], in1=lv[:, 64:128], op=ALU.mult), reads=[lvB], writes=[ltB])
        S.op("dve", lambda e: e.tensor_tensor(out=lt[:, 64:128], in0=lv[:, 128:192], in1=lv[:, 192:256], op=ALU.mult), reads=[lvB, ltB], writes=[ltB])
        S.op("dve", lambda e: e.tensor_reduce(out=lt[:, 128:129], in_=lt[:, 0:64], axis=AX.X, op=ALU.add), reads=[ltB], writes=[ltB])
        S.op("dve", lambda e: e.tensor_reduce(out=lt[:, 129:130], in_=lt[:, 64:128], axis=AX.X, op=ALU.add), reads=[ltB], writes=[ltB])
        S.op("act", lambda e: e.activation(out=lt[:, 130:132], in_=lt[:, 128:130], func=AF.Exp), reads=[ltB], writes=[ltB])
        linit = 0.8 - 0.6 * math.exp(-0.3 * 1)
        S.op("dve", lambda e: e.tensor_tensor(out=lt[:, 132:133], in0=lt[:, 131:132], in1=lt[:, 130:131], op=ALU.subtract), reads=[ltB], writes=[ltB])
        S.op("dve", lambda e: e.tensor_scalar(out=lt[:, 133:134], in0=lt[:, 132:133], scalar1=-linit, scalar2=None, op0=ALU.add), reads=[ltB], writes=[ltB])
        pl = pfv
        plB = pfvB
        S.op("pe", lambda e: e.matmul(pl[:, 0:1], lhsT=onesf[0:1, :], rhs=lt[0:1, 133:134], start=True, stop=True),
             reads=[ltB, constB], writes=[plB])
        nlam = sb("T_nlam", [128, 1], F32, st)
        nlamB = Buf()
        S.op("dve", lambda e: e.tensor_copy(out=nlam[:, :], in_=pl[:, 0:1]), reads=[plB], writes=[nlamB])
        sg = sb("T_sg", [128, 1], F32, st)
        sgB = S.dbuf()
        S.op("sp", lambda e: e.dma_start(out=sg[:, :], in_=b_subln_g), writes=[sgB], dma=sgB)
        sg2 = sb("T_sg2", [128, 1], F32, st)
        S.op("dve", lambda e: e.tensor_scalar(out=sg2[:, :], in0=sg[:, :], scalar1=(1.0 - linit), scalar2=None, op0=ALU.mult),
             reads=[sgB], writes=[sgB])

        KT = sb("T_KT", [128, L], BF16, st)
        Vh = sb("T_V", [128, NT, 128], BF16, st)
        QT = sb("T_QT", [128, NQ], BF16, st)
        kvB = S.dbuf()
        hk = sb("T_hk", [128, 1152], F32, st)
        hkB = S.dbuf()
        strip = sb("T_strip", [128, 1152], F32, st)
        stripB = Buf()
        pS1 = [ps("T_pS1%d" % i, [128, 512], F32, st) for i in range(2)]
        pS2 = [ps("T_pS2%d" % i, [128, 512], F32, st) for i in range(2)]
        pSB_ = [PBuf() for _ in range(4)]
        pO1 = ps("T_pO1", [128, 512], F32, st)
        pO2 = ps("T_pO2", [128, 512], F32, st)
        pOB = [PBuf(), PBuf()]
        pZ1 = pfv
        pZ2 = ps("T_pZ2", [128, 512], F32, st)
        pZB = [pfvB, PBuf()]
        NP = 4
        P1 = [sb("T_P1%d" % i, [128, 512], BF16, st) for i in range(NP)]
        P2 = [sb("T_P2%d" % i, [128, 512], BF16, st) for i in range(NP)]
        PB = [[Buf() for _ in range(NP)] for _ in range(2)]
        tmpb = [sb("T_tmp%d" % i, [128, 512], F32, st) for i in range(2)]
        tmpB = [Buf() for _ in range(2)]
        rz = sb("T_rz", [128, 512], F32, st)
        rzB = Buf()
        a1 = sb("T_a1", [128, 512], F32, st)
        a1B = Buf()
        a2 = sb("T_a2", [128, 512], F32, st)
        a2B = Buf()
        osq = sb("T_osq", [128, 512], F32, st)
        osqB = Buf()
        onb = [sb("T_on%d" % i, [128, 512], BF16, st) for i in range(2)]
        onB = [S.dbuf() for _ in range(2)]
        scale = 64.0 ** -0.5
        it = 0
        oi = 0
        for h in range(8):
            S.op("sp", lambda e, h=h: e.dma_start(out=KT[:, :], in_=s_KT[h, :, :]), writes=[kvB], dma=kvB)
            S.op("sp", lambda e, h=h: e.dma_start(out=Vh[:, :, :], in_=s_V[h, :, :, :]), writes=[kvB], dma=kvB)
            S.op("sp", lambda e, h=h: e.dma_start(out=QT[:, :], in_=s_QT[h, :, :]), writes=[kvB], dma=kvB)
            S.op("sp", lambda e, h=h: e.dma_start(
                out=hk[:, :], in_=bass.AP(tensor=s_fv.tensor, offset=s_fv[h:h + 1, 0:1].offset, ap=[[1, 128], [1, 1152]])),
                writes=[hkB], dma=hkB)
            for c0 in range(0, 1152, 384):
                S.op("pe", lambda e, c0=c0: e.matmul(pO1[:, 0:384], lhsT=Jf, rhs=hk[:, c0:c0 + 384], start=True, stop=True),
                     reads=[cmB, hkB], writes=[pOB[0]])
                S.op("dve", lambda e, c0=c0: e.tensor_copy(out=strip[:, c0:c0 + 384], in_=pO1[:, 0:384]),
                     reads=[pOB[0]], writes=[stripB])
            def emit_S(qg, kb, si):
                q0 = qg * 512
                for w_, (pSx, dlo) in enumerate(((pS1[si], 0), (pS2[si], 64))):
                    S.op("pe", lambda e, pSx=pSx, dlo=dlo: e.matmul(
                        pSx[:, :], lhsT=KT[dlo:dlo + 64, kb * 128:(kb + 1) * 128], rhs=QT[dlo:dlo + 64, q0:q0 + 512],
                        start=True, stop=True), reads=[kvB], writes=[pSB_[si * 2 + w_]])

            flat = [(qg, kb) for qg in range(NQG) for kb in range(NKB)]
            emit_S(flat[0][0], flat[0][1], it % 2)
            for fi, (qg, kb) in enumerate(flat):
                q0 = qg * 512
                if True:
                    o = kb - 4 * qg
                    near = (-1 <= o <= 4)
                    if near:
                        bias_ap = cfar[:, h, 3:4] if kb >= NKB // 2 else zero_c[:, 0:1]
                    elif kb < 4 * qg - 1:
                        bias_ap = cfar[:, h, 0:1]
                    elif kb < NKB // 2:
                        bias_ap = cfar[:, h, 1:2]
                    else:
                        bias_ap = cfar[:, h, 2:3]
                    si = it % 2
                    pslot = it % NP
                    it += 1
                    if fi + 1 < len(flat):
                        emit_S(flat[fi + 1][0], flat[fi + 1][1], it % 2)
                    for w_, (pSx, Px) in enumerate(((pS1[si], P1[pslot]), (pS2[si], P2[pslot]))):
                        sB = pSB_[si * 2 + w_]
                        if near:
                            c0 = 512 - 128 * o
                            S.op("dve", lambda e, pSx=pSx, w_=w_, c0=c0: e.scalar_tensor_tensor(
                                out=tmpb[w_][:, :], in0=pSx[:, :], scalar=scale, in1=strip[:, c0:c0 + 512],
                                op0=ALU.mult, op1=ALU.add), reads=[sB, stripB], writes=[tmpB[w_]])
                            S.op("act", lambda e, Px=Px, w_=w_, bias_ap=bias_ap: e.activation(
                                out=Px[:, :], in_=tmpb[w_][:, :], func=AF.Exp, bias=bias_ap, scale=1.0),
                                reads=[tmpB[w_], cfarB, constB], writes=[PB[w_][pslot]])
                        else:
                            S.op("act", lambda e, Px=Px, pSx=pSx, bias_ap=bias_ap: e.activation(
                                out=Px[:, :], in_=pSx[:, :], func=AF.Exp, bias=bias_ap, scale=scale),
                                reads=[sB, cfarB], writes=[PB[w_][pslot]])
                    for w_, (Px, pOx, pZx) in enumerate(((P1[pslot], pO1, pZ1), (P2[pslot], pO2, pZ2))):
                        S.op("pe", lambda e, Px=Px, pOx=pOx, kb=kb: e.matmul(
                            pOx[:, :], lhsT=Vh[:, kb, :], rhs=Px[:, :], start=(kb == 0), stop=(kb == NKB - 1)),
                            reads=[kvB, PB[w_][pslot]], writes=[pOB[w_]])
                        S.op("pe", lambda e, Px=Px, pZx=pZx, kb=kb: e.matmul(
                            pZx[:, :], lhsT=onesb[:, :], rhs=Px[:, :], start=(kb == 0), stop=(kb == NKB - 1)),
                            reads=[constB, PB[w_][pslot]], writes=[pZB[w_]])
                if kb != NKB - 1:
                    continue
                S.op("dve", lambda e: e.reciprocal(out=rz[:, :], in_=pZ1[:, :]), reads=[pZB[0]], writes=[rzB])
                S.op("dve", lambda e: e.tensor_tensor(out=a1[:, :], in0=pO1[:, :], in1=rz[:, :], op=ALU.mult),
                     reads=[pOB[0], rzB], writes=[a1B])
                S.op("dve", lambda e: e.reciprocal(out=rz[:, :], in_=pZ2[:, :]), reads=[pZB[1], rzB], writes=[rzB])
                S.op("dve", lambda e: e.tensor_tensor(out=a2[:, :], in0=pO2[:, :], in1=rz[:, :], op=ALU.mult),
                     reads=[pOB[1], rzB], writes=[a2B])
                S.op("dve", lambda e: e.scalar_tensor_tensor(out=a1[:, :], in0=a2[:, :], scalar=nlam[:, 0:1], in1=a1[:, :],
                                                             op0=ALU.mult, op1=ALU.add), reads=[a1B, a2B, nlamB], writes=[a1B])
                S.op("act", lambda e: e.activation(out=osq[:, :], in_=a1[:, :], func=AF.Square), reads=[a1B], writes=[osqB])
                S.op("pe", lambda e: e.matmul(pZ2[:, :], lhsT=onesf[:, :], rhs=osq[:, :], start=True, stop=True),
                     reads=[osqB, constB], writes=[pZB[1]])
                S.op("dve", lambda e: e.tensor_scalar(out=a2[:, :], in0=pZ2[:, :], scalar1=1.0 / 128.0, scalar2=EPS,
                                                      op0=ALU.mult, op1=ALU.add), reads=[pZB[1]], writes=[a2B])
                S.op("act", lambda e: e.sqrt(out=a2[:, :], in_=a2[:, :]), reads=[a2B], writes=[a2B])
                S.op("dve", lambda e: e.reciprocal(out=a2[:, :], in_=a2[:, :]), reads=[a2B], writes=[a2B])
                ob = onb[oi % 2]; obB = onB[oi % 2]; oi += 1
                S.op("dve", lambda e, ob=ob: e.scalar_tensor_tensor(out=ob[:, :], in0=a1[:, :], scalar=sg2[:, 0:1], in1=a2[:, :],
                                                                   op0=ALU.mult, op1=ALU.mult), reads=[a1B, a2B, sgB], writes=[obB])
                S.op("sp", lambda e, ob=ob, h=h, q0=q0: e.dma_start(out=s_oT[h, :, q0:q0 + 512], in_=ob[:, :]),
                     reads=[obB], dma=obB)
        barrier()

    with contextlib.ExitStack() as st:
      if phase_on():
        cx = mk_cross(st, 1, "C4")
        Wo = sb("C4_Wo", [128, 8, D], BF16, st)
        WoB = Buf()
        load_w(b_w_out, D, D, Wo, WoB)
        xs = [sb("C4_x%d" % i, [128, D], F32, st) for i in range(2)]
        xsB = [S.dbuf() for _ in range(2)]
        oT = [sb("C4_oT%d" % i, [128, 8, 128], BF16, st) for i in range(2)]
        oTB = [S.dbuf() for _ in range(2)]
        pw = ps("C4_pw", [128, 512], F32, st)
        pwB = PBuf()

        def loadt4(ti):
            i = ti % 2
            r0 = ti * 128
            S.op("sp", lambda e: e.dma_start(out=xs[i][:, :], in_=s_x2[r0:r0 + 128, :]), writes=[xsB[i]], dma=xsB[i])
            S.op("sp", lambda e: e.dma_start(out=oT[i][:, :, :], in_=s_oT[:, :, r0:r0 + 128].rearrange("h p t -> p h t")),
                 writes=[oTB[i]], dma=oTB[i])

        loadt4(0)
        for ti in range(NQT):
            i = ti % 2
            if ti + 1 < NQT:
                loadt4(ti + 1)
            for hv in range(2):
                for c in range(8):
                    S.op("pe", lambda e, hv=hv, c=c: e.matmul(pw[:, :], lhsT=oT[i][:, c, :], rhs=Wo[:, c, hv * 512:(hv + 1) * 512],
                                                              start=(c == 0), stop=(c == 7)), reads=[WoB, oTB[i]], writes=[pwB])
                S.op("dve", lambda e, hv=hv: e.tensor_tensor(out=xs[i][:, hv * 512:(hv + 1) * 512], in0=pw[:, :],
                                                             in1=xs[i][:, hv * 512:(hv + 1) * 512], op=ALU.add),
                     reads=[pwB, xsB[i]], writes=[xsB[i]])
            cross_attn(cx, xs[i], xsB[i], 0)
            S.op("sp", lambda e, ti=ti: e.dma_start(out=s_x3[ti * 128:(ti + 1) * 128, :], in_=xs[i][:, :]),
                 reads=[xsB[i]], dma=xsB[i])
        barrier()

    if phase_on():
        mlp_phase(1, s_x3, y_out, NQT, True)

    S.emit()
    stack.close()
    return nc


def _rel_bucket_np(rp):
    import jax.numpy as jnp
    half = 16
    max_exact = 8
    rp = jnp.asarray(rp, dtype=jnp.int32)
    ret = (rp > 0).astype(jnp.int32) * half
    n = jnp.abs(rp)
    nf = jnp.maximum(n, 1).astype(jnp.float32)
    large = max_exact + (jnp.log(nf / max_exact) / math.log(128 / max_exact) * (half - max_exact)).astype(jnp.int32)
    large = jnp.minimum(large, half - 1)
    return np.asarray(ret + jnp.where(n < max_exact, n, large))


def _consts(reverse, two_seg):
    cm = np.zeros((128, 6, 128), np.float32)
    j = np.arange(128)
    cm[:, 0, :] = np.eye(128)
    cm[:, 1, :] = np.eye(128)[::-1]
    cm[:, 2, :] = (j[:, None] <= j[None, :])
    cm[:, 3, :] = (j[:, None] >= j[None, :])
    cm[:, 4, :] = np.where(j[:, None] <= j[None, :], 0.0, NEG)
    cm[:, 5, :] = np.where(j[:, None] >= j[None, :], 0.0, NEG)
    xs = np.arange(1280)
    delta = 639 - xs
    sgn = -1 if reverse else 1
    bk = _rel_bucket_np(sgn * delta)
    oh = np.zeros((32, 1408), np.float32)
    oh[bk, xs] = 1.0
    b_before = int(_rel_bucket_np(np.array([sgn * -100000]))[0])
    b_after = int(_rel_bucket_np(np.array([sgn * 100000]))[0])
    oh[b_before, 1280] = 1.0
    oh[b_after, 1281] = 1.0
    fl = np.zeros((128, 4), np.float32)
    fl[:, 0] = 0.0 if two_seg else 1.0
    fl[:, 1] = NEG if two_seg else 0.0
    return cm, oh, fl


_PROG = {}


def _get_prog(L):
    if L not in _PROG:
        _PROG[L] = build_program(L)
    return _PROG[L]


def make_core_inputs(stream_x, mems, reverse, two_seg, w):
    x = np.ascontiguousarray(stream_x[::-1] if reverse else stream_x, dtype=np.float32)
    mm = np.ascontiguousarray(mems[::-1] if reverse else mems, dtype=np.float32)
    wg = w["a_w_gate"][0]
    bgate = w["a_b_gate"][0]
    if reverse:
        perm = np.concatenate([np.arange(8, 16), np.arange(0, 8)])
        wg = wg[:, perm]
        bgate = bgate[perm]
    cm, oh, fl = _consts(reverse, two_seg)
    gains = np.stack([w["g_mix"][0], w["g_mix"][1], w["g_cross"][0], w["g_cross"][1], w["g_mem"][0], w["g_mem"][1],
                      w["g_mlp"][0], w["g_mlp"][1], w["g_final"]]).astype(np.float32)
    f32 = lambda a: np.ascontiguousarray(a, dtype=np.float32)
    return {
        "x": x, "mem": mm, "gains": gains,
        "a_w_in": f32(w["a_w_in"][0]), "a_w_gate": f32(wg), "a_b_gate": f32(bgate.reshape(1, 16)),
        "a_norm_g": f32(w["a_norm_g"][0].reshape(1, D)), "a_w_out": f32(w["a_w_out"][0]),
        "b_w_qkv": f32(w["b_w_qkv"][0]), "b_lambda": f32(w["b_lambda"][0].reshape(1, 256)),
        "b_subln_g": f32(w["b_subln_g"][0].reshape(128, 1)), "b_w_out": f32(w["b_w_out"][0]),
        "rel_bias": f32(w["rel_bias"]), "c_w_q": f32(w["c_w_q"]), "c_w_kv": f32(w["c_w_kv"]),
        "c_w_out": f32(w["c_w_out"]), "f_w1": f32(w["f_w1"]), "f_w2": f32(w["f_w2"]),
        "cmat": cm, "onehot": oh, "flags": fl,
    }


def kernel(x_prompt, x_sample, mem_prompt, mem_sample, **w):
    x_prompt = np.asarray(x_prompt); x_sample = np.asarray(x_sample)
    mem_prompt = np.asarray(mem_prompt); mem_sample = np.asarray(mem_sample)
    w = {k: np.asarray(v) for k, v in w.items()}
    L = x_prompt.shape[1]
    NQ = L // 2
    nc = _get_prog(L)
    in_maps = [None] * 8
    for b in range(2):
        mems = np.stack([mem_prompt[b], mem_prompt[b]])
        in_maps[b] = make_core_inputs(x_prompt[b], mems, False, False, w)
        in_maps[b + 4] = make_core_inputs(x_prompt[b], mems, True, False, w)
    for p in range(2):
        xs = np.concatenate([x_sample[2 * p], x_sample[2 * p + 1]], axis=0)
        mems = np.stack([mem_sample[2 * p], mem_sample[2 * p + 1]])
        in_maps[2 + p] = make_core_inputs(xs, mems, False, True, w)
        in_maps[6 + p] = make_core_inputs(xs, mems, True, True, w)
    res = run_bass_kernel_spmd(nc, in_maps, core_ids=list(range(8)))
    ys = [np.asarray(r["y"], dtype=np.float32) for r in res.results]
    y_prompt = np.stack([np.concatenate([ys[b], ys[b + 4][::-1]], axis=0) for b in range(2)])
    y_sample = np.stack([ys[2], ys[6][::-1], ys[3], ys[7][::-1]])
    return (y_prompt, y_sample)
```
